# Optimizing a Trainium2 kernel written in Bass

```python
import jax, jax.numpy as jnp
from jax import lax
import numpy as np

D_MODEL = 2048
BATCH = 4
SEQ = 4096
DEPTH = 1

CHUNK = 64
PLE_DIM = 256
D_CONV = D_MODEL
CONV_WIDTH = 3
RET_HEADS = 8
RET_DK = D_MODEL // RET_HEADS
RET_DV = 2 * RET_DK
RET_QK = RET_HEADS * RET_DK
RET_V = RET_HEADS * RET_DV
D_FF = -(-8 * D_MODEL // (3 * 256)) * 256
ROPE_BASE = 10000.0
EPS = 1e-6
SPLITS = (D_CONV, D_CONV, D_CONV, RET_QK, RET_QK, RET_V, RET_V, D_MODEL, D_MODEL)
N_IN = sum(SPLITS)

kernel_name = "hybrid_shortconv_retention_block"


def rms_norm(x, g):
    xf = x.astype(jnp.float32)
    y = xf * lax.rsqrt(jnp.mean(xf * xf, axis=-1, keepdims=True) + EPS)
    return (y * g.astype(jnp.float32)).astype(x.dtype)


def rotary(t, pos):
    half = t.shape[-1] // 2
    inv = ROPE_BASE ** (-jnp.arange(half, dtype=jnp.float32) / half)
    ang = pos.astype(jnp.float32)[:, None] * inv[None, :]
    cos = jnp.cos(ang)[None, :, None, :]
    sin = jnp.sin(ang)[None, :, None, :]
    tf = t.astype(jnp.float32)
    t1, t2 = tf[..., :half], tf[..., half:]
    return jnp.concatenate([t1 * cos - t2 * sin, t2 * cos + t1 * sin], axis=-1).astype(t.dtype)


def short_conv_mixer(b, c, v, conv_w):
    u = c * v
    s = u.shape[1]
    up = jnp.pad(u, ((0, 0), (CONV_WIDTH - 1, 0), (0, 0)))
    y = conv_w[0] * up[:, 0:s]
    for tap in range(1, CONV_WIDTH):
        y = y + conv_w[tap] * up[:, tap:tap + s]
    return b * y


def retention(q, k, v):
    bsz, s = q.shape[0], q.shape[1]
    n = s // CHUNK
    log_gamma = jnp.log1p(-jnp.exp2(-5.0 - jnp.arange(RET_HEADS, dtype=jnp.float32)))
    idx = jnp.arange(CHUNK, dtype=jnp.float32)
    d_intra = jnp.exp(log_gamma[:, None, None] * jnp.abs(idx[:, None] - idx[None, :]))
    xi = jnp.exp(log_gamma[:, None] * (idx + 1.0))
    zeta = jnp.exp(log_gamma[:, None] * (CHUNK - 1.0 - idx))
    g_chunk = jnp.exp(log_gamma * CHUNK)

    def to_chunks(t):
        return t.reshape(bsz, n, CHUNK, t.shape[2], t.shape[3]).transpose(0, 3, 1, 2, 4)

    qc, kc, vc = to_chunks(q), to_chunks(k), to_chunks(v)
    scores = jnp.einsum('bhncd,bhnmd->bhncm', qc, kc) * d_intra[None, :, None]
    o_intra = jnp.einsum('bhncm,bhnme->bhnce', scores, vc)

    def step(state, inp):
        q_i, k_i, v_i = inp
        o = jnp.einsum('bhcd,bhde->bhce', q_i, state) * xi[None, :, :, None]
        state = state * g_chunk[None, :, None, None] + jnp.einsum(
            'bhcd,bhce->bhde', k_i * zeta[None, :, :, None], v_i)
        return state, o

    xs = (qc.transpose(2, 0, 1, 3, 4), kc.transpose(2, 0, 1, 3, 4), vc.transpose(2, 0, 1, 3, 4))
    state0 = jnp.zeros((bsz, RET_HEADS, q.shape[-1], v.shape[-1]), jnp.float32)
    _, o_cross = lax.scan(step, state0, xs)
    o = o_intra + o_cross.transpose(1, 2, 0, 3, 4)
    return o.transpose(0, 2, 3, 1, 4).reshape(bsz, s, RET_HEADS, v.shape[-1])


def head_group_norm(o, g, dtype):
    of = o.astype(jnp.float32)
    mu = jnp.mean(of, axis=-1, keepdims=True)
    var = jnp.mean(jnp.square(of - mu), axis=-1, keepdims=True)
    y = (of - mu) * lax.rsqrt(var + EPS)
    y = y.reshape(o.shape[0], o.shape[1], RET_V) * g.astype(jnp.float32)
    return y.astype(dtype)


def setup_inputs(seed: int = 0) -> dict:
    key = jax.random.key(seed)
    ks = jax.random.split(key, 18)
    f32 = jnp.float32

    def w(k, shape, fan_in):
        return jax.random.normal(k, shape, f32) * (fan_in ** -0.5)

    def gain(k, shape):
        return 1.0 + 0.01 * jax.random.normal(k, shape, f32)

    return {
        "x": jax.random.normal(ks[0], (BATCH, SEQ, D_MODEL), f32),
        "p": jax.random.normal(ks[1], (DEPTH, BATCH, SEQ, PLE_DIM), f32),
        "g_mix": gain(ks[2], (DEPTH, D_MODEL)),
        "w_in": w(ks[3], (DEPTH, D_MODEL, N_IN), D_MODEL),
        "conv_w": w(ks[4], (DEPTH, CONV_WIDTH, D_CONV), CONV_WIDTH),
        "w_conv_out": w(ks[5], (DEPTH, D_CONV, D_MODEL), D_CONV),
        "g_ret": gain(ks[6], (DEPTH, RET_V)),
        "w_ret_out": w(ks[7], (DEPTH, RET_V, D_MODEL), RET_V),
        "w_o": w(ks[8], (DEPTH, D_MODEL, D_MODEL), D_MODEL),
        "g_ffn": gain(ks[9], (DEPTH, D_MODEL)),
        "w_ffn_in": w(ks[10], (DEPTH, D_MODEL, 2 * D_FF), D_MODEL),
        "w_ffn_out": w(ks[11], (DEPTH, D_FF, D_MODEL), D_FF),
        "g_ple": gain(ks[12], (DEPTH, D_MODEL)),
        "w_ple_gate": w(ks[13], (DEPTH, D_MODEL, D_MODEL), D_MODEL),
        "w_ple_proj": w(ks[14], (DEPTH, PLE_DIM, D_MODEL), PLE_DIM),
        "g_final": gain(ks[15], (D_MODEL,)),
    }


def reference(x, p, g_mix, w_in, conv_w, w_conv_out, g_ret, w_ret_out, w_o,
              g_ffn, w_ffn_in, w_ffn_out, g_ple, w_ple_gate, w_ple_proj, g_final):
    bsz, s, _ = x.shape
    pos = jnp.arange(s, dtype=jnp.int32)
    split_points = np.cumsum(SPLITS)[:-1].tolist()
    for i in range(DEPTH):
        h = rms_norm(x, g_mix[i])
        proj = h @ w_in[i]
        cb, cc, cv, q, k, v, g, gate_conv, gate_ret = jnp.split(proj, split_points, axis=-1)

        y_conv = short_conv_mixer(cb, cc, cv, conv_w[i]) @ w_conv_out[i]

        q = rotary(q.reshape(bsz, s, RET_HEADS, RET_DK), pos)
        k = rotary(k.reshape(bsz, s, RET_HEADS, RET_DK), pos) * (RET_DK ** -0.5)
        o = retention(q, k, v.reshape(bsz, s, RET_HEADS, RET_DV))
        o = head_group_norm(o, g_ret[i], x.dtype)
        y_ret = (jax.nn.silu(g) * o) @ w_ret_out[i]

        merged = jax.nn.sigmoid(gate_conv) * y_conv + jax.nn.sigmoid(gate_ret) * y_ret
        x = x + merged @ w_o[i]

        h = rms_norm(x, g_ffn[i])
        a, b = jnp.split(h @ w_ffn_in[i], 2, axis=-1)
        x = x + (jax.nn.silu(a) * b) @ w_ffn_out[i]

        ple_gate = jax.nn.sigmoid(rms_norm(x, g_ple[i]) @ w_ple_gate[i])
        x = x + ple_gate * (p[i] @ w_ple_proj[i])
    return rms_norm(x, g_final)
```

```python
import numpy as np
import ml_dtypes
from contextlib import ExitStack
import concourse.bass as bass
import concourse.mybir as mybir
from concourse.bass_utils import run_bass_kernel_spmd

F32 = mybir.dt.float32
BF16 = mybir.dt.bfloat16
AF = mybir.ActivationFunctionType
ALU = mybir.AluOpType

D = 2048
T = 512
NT = 4
TOK = 2048
H = 8
DK = 256
DV = 512
DFF = 5632
PLE = 256
NIN = 22528
EPS = 1e-6
OB, OC, OV, OQ, OK_, OVV, OG, OGC, OGR = 0, 2048, 4096, 6144, 8192, 10240, 14336, 18432, 20480
NSLOT = 2
NBANK = 8
NPAN = 112


class Buf:
    __slots__ = ("name", "w", "r", "const")

    def __init__(self, name, const=False, gate=()):
        self.name = name
        self.w = None
        self.r = list(gate)
        self.const = const


class Sched:
    ENGS = ("sync", "scalar", "vector", "gpsimd", "tensor")

    def __init__(self):
        self.prog = {e: [] for e in self.ENGS}
        self.sems = {}
        self.count = {}
        self.waited = {e: {} for e in self.ENGS}
        self.phase_bufs = []
        self.gate = []

    def add_sem(self, key, handle):
        self.sems[key] = handle
        self.count[key] = 0

    def _deps(self, eng, reads, writes, extra=()):
        deps = {}

        def add(tok):
            if tok is None:
                return
            k, v = tok
            if deps.get(k, 0) < v:
                deps[k] = v
        for b in reads:
            add(b.w)
        for b in writes:
            add(b.w)
            for t in b.r:
                add(t)
        for t in extra:
            add(t)
        waits = []
        wd = self.waited[eng]
        for k, v in deps.items():
            if wd.get(k, 0) < v:
                wd[k] = v
                waits.append((k, v))
        return waits

    def _mark(self, tok, reads, writes):
        for b in reads:
            if not b.const:
                b.r.append(tok)
        for b in writes:
            b.w = tok
            b.r = []

    def op(self, eng, fn, reads=(), writes=(), extra=()):
        waits = self._deps(eng, reads, writes, extra)
        self.count[eng] += 1
        tok = (eng, self.count[eng])
        self.prog[eng].append((waits, [fn], (eng, 1)))
        self._mark(tok, reads, writes)
        return tok

    def dma(self, eng, fns, semkey, reads=(), writes=(), extra=()):
        waits = self._deps(eng, reads, writes, extra)
        if not isinstance(fns, (list, tuple)):
            fns = [fns]
        self.count[semkey] += 16 * len(fns)
        tok = (semkey, self.count[semkey])
        self.prog[eng].append((waits, list(fns), (semkey, 16)))
        self._mark(tok, reads, writes)
        return tok

    def cc(self, eng, fn, semkey, reads=(), writes=()):
        waits = self._deps(eng, reads, writes)
        self.count[semkey] += 1
        tok = (semkey, self.count[semkey])
        self.prog[eng].append((waits, [fn], (semkey, 1)))
        self._mark(tok, reads, writes)
        return tok

    def wait_all(self, eng, toks):
        waits = self._deps(eng, (), (), toks)
        self.prog[eng].append((waits, [], None))

    def pbuf(self, name):
        b = Buf(name, gate=self.gate)
        self.phase_bufs.append(b)
        return b

    def new_phase(self, keep=()):
        mx = {}
        for b in self.phase_bufs:
            toks = list(b.r)
            if b.w is not None:
                toks.append(b.w)
            for k, v in toks:
                if mx.get(k, 0) < v:
                    mx[k] = v
        for k, v in self.gate:
            if mx.get(k, 0) < v:
                mx[k] = v
        self.gate = list(mx.items())
        self.phase_bufs = list(keep)

    def emit(self, block):
        S = self

        def mk(ename):
            def body(e):
                for waits, fns, inc in S.prog[ename]:
                    for k, v in waits:
                        e.wait_ge(S.sems[k], v)
                    for fn in fns:
                        inst = fn(e)
                        if inc is not None and (inc[1] == 16):
                            inst.then_inc(S.sems[inc[0]], 16)
                    if inc is not None and inc[1] == 1 and fns:
                        inst.then_inc(S.sems[inc[0]], 1)
            return body
        block.sync(mk("sync"))
        block.scalar(mk("scalar"))
        block.vector(mk("vector"))
        block.gpsimd(mk("gpsimd"))
        block.tensor(mk("tensor"))


def build_nc(gam128):
    nc = bass.Bass("TRN2", target_bir_lowering=False)

    def din(name, shape):
        return nc.dram_tensor(name, list(shape), F32, kind="ExternalInput").ap()
    x_d = din("x", [TOK, D])
    p_d = din("p", [TOK, PLE])
    w_in = din("w_in", [D, NIN])
    w_co = din("w_conv_out", [D, D])
    w_ro = din("w_ret_out", [H * DV, D])
    w_o = din("w_o", [D, D])
    w_fi = din("w_ffn_in", [D, 2 * DFF])
    w_fo = din("w_ffn_out", [DFF, D])
    w_pg = din("w_ple_gate", [D, D])
    w_pp = din("w_ple_proj", [PLE, D])
    cs_d = din("cs", [4, 128, TOK])
    mask_d = din("maskT", [128, H * 128])
    small_d = din("small", [128, 256])
    ident_d = din("ident", [128, 128])
    out_d = nc.dram_tensor("out", [TOK, D], F32, kind="ExternalOutput").ap()
    scr_d = nc.dram_tensor("scr", [NPAN, 128, 8192], BF16, kind="Internal").ap()
    kvs_d = nc.dram_tensor("kvs", [NT * H, 128, 4096], BF16, kind="Internal").ap()
    xch = [(nc.dram_tensor(f"xin{i}", [r, w], F32), nc.dram_tensor(f"xout{i}", [2 * r, w], F32)) for i, (r, w) in enumerate(((1024, 512), (1024, 512), (128, 32)))]

    with ExitStack() as es:
        def sb(name, shape, dt):
            return es.enter_context(nc.sbuf_tensor("sb_" + name, list(shape), dt))
        xT = sb("xT", [128, 16, T], F32)
        hT = sb("hT", [128, 16, T], BF16)
        wsl = [sb(f"wsl{i}", [128, 8192], BF16) for i in range(NSLOT)]
        S32 = sb("S32", [128, 16, 512], F32)
        cst = sb("cst", [128, 2, T], F32)
        mask = sb("mask", [128, H, 128], F32)
        small = sb("small", [128, 256], F32)
        identf = sb("identf", [128, 128], F32)
        identb = sb("identb", [128, 128], BF16)
        ones = sb("ones", [128, 128], BF16)
        sq = sb("sq", [128, 4, T], BF16)
        rstd = sb("rstd", [128, T], F32)
        ucarry = sb("ucarry", [128, 16, 2], F32)
        epsc = sb("epsc", [128, 2], F32)
        PB = sb("PB", [128, 30976], BF16)
        PF = sb("PF", [128, 4640], F32)
        banks = [es.enter_context(nc.psum_tensor(f"psbank{i}", [128, 512], F32)) for i in range(NBANK)]

        S = Sched()
        semnames = list(Sched.ENGS) + [f"dw{i}" for i in range(NSLOT)] + [f"dsw{i}" for i in range(NSLOT)] + ["dkv", "dks0", "dks1", "dxi0", "dxi1", "dxi2","dpc0", "dpc1", "dpc2", "dpc3", "dpc4", "dpc5", "dpc6", "dpc7", "dpc8", "dpc9", "dpc10", "dpc11", "dpc12", "dpc13", "dpc14", "dpc15", "dxo0", "dxo1", "dxo2", "cc0", "cc1", "cc2", "dx0", "dx1", "dc", "dcs", "dp", "do0", "do1", "dwp"]
        for k in semnames:
            S.add_sem(k, es.enter_context(nc.semaphore("sem_" + k)))
        block = es.enter_context(nc.Block())

        BxT = [Buf(f"xT{k}") for k in range(16)]
        BhT = [Buf(f"hT{k}") for k in range(16)]
        Bw = [Buf(f"w{i}") for i in range(NSLOT)]
        BS32 = [Buf(f"S32_{i}") for i in range(16)]
        Bcst = Buf("cst")
        Bconst = Buf("const", const=True)
        Bsq = [Buf(f"sq{i}") for i in range(4)]
        Brstd = Buf("rstd")
        Bucar = Buf("ucarry")
        Bbank = [Buf(f"bank{i}") for i in range(NBANK)]
        st = {"bank": 0, "slot": 0, "ev": 0}

        gcol = lambda which, kc: small[:, which * 16 + kc: which * 16 + kc + 1]
        gret = lambda j: small[:, 64 + j: 65 + j]
        cwc = lambda tap, kc: small[:, 96 + tap * 16 + kc: 97 + tap * 16 + kc]
        kzs4 = lambda h, b: small[:, 161 + h * 4 + b: 162 + h * 4 + b]
        offc = lambda h, d: small[:, 193 + h * 3 + d - 1: 194 + h * 3 + d - 1]
        epsx = lambda h: small[:, 152 + h: 153 + h]

        def nb():
            i = st["bank"]
            st["bank"] = (i + 1) % (NBANK - 1)
            return banks[i], Bbank[i]

        def ev_eng():
            st["ev"] ^= 1
            return "scalar" if st["ev"] else "vector"

        def copy_op(eng, out, in_, reads, writes):
            if eng == "scalar":
                return S.op("scalar", lambda e: e.activation(out=out, in_=in_, func=AF.Copy), reads=reads, writes=writes)
            return S.op("vector", lambda e: e.tensor_copy(out=out, in_=in_), reads=reads, writes=writes)

        S.dma("sync", [lambda e: e.dma_start(out=mask[:].rearrange("p h c -> p (h c)"), in_=mask_d),
                       lambda e: e.dma_start(out=small[:], in_=small_d),
                       lambda e: e.dma_start(out=identf[:], in_=ident_d)], "dc", writes=[Bconst])
        S.op("vector", lambda e: e.tensor_copy(out=identb[:], in_=identf[:]), reads=[Bconst], writes=[Bconst])
        S.op("vector", lambda e: e.memset(ones[:], 1.0 / D), writes=[Bconst])
        S.op("vector", lambda e: e.memset(epsc[:], EPS), writes=[Bconst])
        S.op("vector", lambda e: e.memset(S32[:].rearrange("p a b -> p (a b)"), 0.0), writes=BS32)
        S.op("vector", lambda e: e.memset(ucarry[:].rearrange("p a b -> p (a b)"), 0.0), writes=[Bucar])

        scr_ids = {}
        Bscr = {}
        scr_ok = {}
        wb_at = {}
        cur = {"ph": "A", "t": 0, "nmain": 0}

        def load_panel(wap, KC, colranges, rowq=0):
            s = st["slot"]
            st["slot"] = (s + 1) % NSLOT
            ncols = sum(c1 - c0 for c0, c1 in colranges)
            assert KC * ncols <= 8192
            flat = wsl[s][:, 0:KC * ncols]
            view = flat.rearrange("p (kc n) -> p kc n", kc=KC)
            key = (wap.name, rowq, KC, tuple(colranges))
            now = (cur["ph"], cur["t"])
            if key in scr_ids and scr_ok.get(key):
                pid = scr_ids[key]
                S.dma("sync", lambda e: e.dma_start(out=flat, in_=scr_d[pid, :, 0:KC * ncols]), f"dw{s}", reads=[Bscr[pid]], writes=[Bw[s]])
                return view, Bw[s]
            wv = wap.rearrange("(kc p) n -> p kc n", p=128)
            fns = []
            off = 0
            for c0, c1 in colranges:
                n = c1 - c0
                fns.append(lambda e, off=off, n=n, c0=c0, c1=c1: e.dma_start(out=view[:, :, off:off + n], in_=wv[:, :, c0:c1]))
                off += n
            S.dma("gpsimd", fns, f"dw{s}", writes=[Bw[s]])
            if key not in scr_ids:
                pid = len(scr_ids)
                assert pid < NPAN
                scr_ids[key] = pid
                Bscr[pid] = Buf(f"scr{pid}")
                tgt = now
                wb_at[key] = tgt
            if wb_at[key] == now:
                pid = scr_ids[key]
                S.dma("sync", lambda e: e.dma_start(out=scr_d[pid, :, 0:KC * ncols], in_=flat), f"dsw{s}", reads=[Bw[s]], writes=[Bscr[pid]])
                scr_ok[key] = True
            return view, Bw[s]

        npre = [0]

        def preconvert(wap, KC, colranges, rowq=0):
            key = (wap.name, rowq, KC, tuple(colranges))
            if key in scr_ids:
                return
            ncols = sum(c1 - c0 for c0, c1 in colranges)
            pid = len(scr_ids)
            assert pid < NPAN
            scr_ids[key] = pid
            Bscr[pid] = Buf(f"scr{pid}")
            wv = wap.rearrange("(kc p) n -> p kc n", p=128)
            dst = scr_d[pid, :, 0:KC * ncols].rearrange("p (kc n) -> p kc n", kc=KC)
            fns = []
            off = 0
            for c0, c1 in colranges:
                n = c1 - c0
                fns.append(lambda e, off=off, n=n, c0=c0, c1=c1: e.dma_start(out=dst[:, :, off:off + n], in_=wv[:, :, c0:c1]))
                off += n
            S.dma("gpsimd", fns, f"dpc{npre[0]}", writes=[Bscr[pid]])
            npre[0] += 1
            scr_ok[key] = True
            wb_at[key] = None

        pre_list = ([(w_in, 16, [(OC + j * 512, OC + (j + 1) * 512)]) for j in range(4)]
                    + [(w_in, 16, [(OV + j * 512, OV + (j + 1) * 512)]) for j in range(4)])

        def group_fm(view, Bv, n, KC, rhs_fn, rbufs, N=T):
            bk, Bb = nb()
            def fn(e):
                for kc in range(KC):
                    inst = e.matmul(bk[:, 0:N], lhsT=view[:, kc, n * 128:(n + 1) * 128], rhs=rhs_fn(kc),
                                    start=(kc == 0), stop=(kc == KC - 1))
                return inst
            S.op("tensor", fn, reads=[Bv] + list(rbufs), writes=[Bb])
            flush_stats()
            return bk, Bb

        def group_tm(view, Bv, blk, KC, ncols, lhs_fn, rbufs):
            bk, Bb = nb()
            def fn(e):
                for kc in range(KC):
                    inst = e.matmul(bk[:, 0:ncols], lhsT=lhs_fn(kc, blk), rhs=view[:, kc, 0:ncols],
                                    start=(kc == 0), stop=(kc == KC - 1))
                return inst
            S.op("tensor", fn, reads=[Bv] + list(rbufs), writes=[Bb])
            return bk, Bb

        def proj_ksplit(wap, KCT, kcp, col0, rhs_of_kc, rbufs, evac):
            nq = KCT // kcp
            bks = [nb() for _ in range(4)]
            for q in range(nq):
                view, Bv = load_panel(wap[q * kcp * 128:(q + 1) * kcp * 128, :], kcp, [(col0, col0 + 512)], rowq=q + 1)
                for n in range(4):
                    bk, Bb = bks[n]
                    def fn(e, q=q, n=n, bk=bk, view=view):
                        for kc in range(kcp):
                            inst = e.matmul(bk[:], lhsT=view[:, kc, n * 128:(n + 1) * 128], rhs=rhs_of_kc(q * kcp + kc),
                                            start=(q == 0 and kc == 0), stop=(q == nq - 1 and kc == kcp - 1))
                        return inst
                    S.op("tensor", fn, reads=[Bv] + list(rbufs), writes=[Bb])
                    flush_stats()
            for n in range(4):
                evac(n, bks[n][0], bks[n][1])

        hT_rhs = lambda kc: hT[:, kc, :]
        hT_lhs = lambda kc, blk: hT[:, kc, blk * 128:(blk + 1) * 128]

        def load_x_tile(src, t):
            S.new_phase()
            xst = [PF[:, i * 2048:(i + 1) * 2048] for i in range(2)]
            Bxst = [S.pbuf(f"xst{i}") for i in range(2)]
            for blk in range(4):
                r = blk % 2
                r0 = t * T + blk * 128
                S.dma("sync", lambda e, r=r, r0=r0: e.dma_start(out=xst[r], in_=src[r0:r0 + 128, :]), f"dx{r}", writes=[Bxst[r]])
                for g in range(4):
                    bk, Bb = nb()
                    def fn(e, r=r, g=g, bk=bk):
                        for j in range(4):
                            kc = g * 4 + j
                            inst = e.transpose(out=bk[:, j * 128:(j + 1) * 128], in_=xst[r][:, kc * 128:(kc + 1) * 128], identity=identf[:])
                        return inst
                    S.op("tensor", fn, reads=[Bxst[r], Bconst], writes=[Bb])
                    copy_op(ev_eng(), xT[:, g * 4:(g + 1) * 4, blk * 128:(blk + 1) * 128],
                            bk[:].rearrange("p (j c) -> p j c", j=4), [Bb], BxT[g * 4:(g + 1) * 4])

        def load_cs(which, t):
            S.dma("sync", [lambda e: e.dma_start(out=cst[:, 0, :], in_=cs_d[2 * which, :, t * T:(t + 1) * T]),
                           lambda e: e.dma_start(out=cst[:, 1, :], in_=cs_d[2 * which + 1, :, t * T:(t + 1) * T])],
                  "dcs", writes=[Bcst])

        stats = {"n": 0, "pend": []}
        sbank, Bsbank = banks[NBANK - 1], Bbank[NBANK - 1]

        def stats_chunk(kc):
            i = stats["n"]
            r = i % 4
            S.op("scalar", lambda e, kc=kc, r=r: e.activation(out=sq[:, r, :], in_=xT[:, kc, :], func=AF.Square),
                 reads=[BxT[kc]], writes=[Bsq[r]])
            def mm(i=i, r=r):
                S.op("tensor", lambda e: e.matmul(sbank[:], lhsT=ones[:], rhs=sq[:, r, :], start=(i == 0), stop=(i == 15)),
                     reads=[Bsq[r], Bconst], writes=([Bsbank] if i in (0, 15) else []))
            stats["pend"].append(mm)
            stats["n"] += 1

        def flush_stats():
            while stats["pend"]:
                stats["pend"].pop(0)()

        def norm(which, inplace=False):
            if stats["n"] == 0:
                for kc in range(16):
                    stats_chunk(kc)
                    flush_stats()
            flush_stats()
            assert stats["n"] == 16
            stats["n"] = 0
            S.op("scalar", lambda e: e.activation(out=rstd[:], in_=sbank[:], func=AF.Sqrt, bias=epsc[:, 0:1], scale=1.0),
                 reads=[Bsbank, Bconst], writes=[Brstd])
            S.op("vector", lambda e: e.reciprocal(out=rstd[:], in_=rstd[:]), reads=[Brstd], writes=[Brstd])
            for kc in range(16):
                if inplace:
                    S.op("vector", lambda e, kc=kc: e.scalar_tensor_tensor(out=xT[:, kc, :], in0=xT[:, kc, :], scalar=gcol(which, kc), in1=rstd[:], op0=ALU.mult, op1=ALU.mult),
                         reads=[Brstd, Bconst], writes=[BxT[kc]])
                else:
                    S.op("vector", lambda e, kc=kc: e.scalar_tensor_tensor(out=hT[:, kc, :], in0=xT[:, kc, :], scalar=gcol(which, kc), in1=rstd[:], op0=ALU.mult, op1=ALU.mult),
                         reads=[BxT[kc], Brstd, Bconst], writes=[BhT[kc]])

        def alloc_ret():
            R = {}
            R["m"] = PB[:, 0:16384].rearrange("p (c t) -> p c t", c=32)
            R["qT"] = PB[:, 16384:17408].rearrange("p (c t) -> p c t", c=2)
            R["kT"] = PB[:, 17408:18432].rearrange("p (c t) -> p c t", c=2)
            R["kz"] = PB[:, 18432:19456].rearrange("p (b d) -> p b d", b=4)
            R["v"] = PB[:, 19456:21504].rearrange("p (b e) -> p b e", b=4)
            R["sg"] = PB[:, 21504:23552].rearrange("p (c t) -> p c t", c=4)
            R["Sbf"] = [PB[:, 23552 + i * 1024:23552 + (i + 1) * 1024].rearrange("p (c e) -> p c e", c=2) for i in range(4)]
            pairs = [(kb, qb) for kb in range(4) for qb in range(kb, 4)]
            R["PT"] = {pr: PB[:, 27648 + i * 128:27648 + (i + 1) * 128] for i, pr in enumerate(pairs)}
            R["on"] = [PB[:, 28928 + i * 512:28928 + (i + 1) * 512] for i in range(4)]
            R["stage"] = [PF[:, i * 1024:(i + 1) * 1024].rearrange("p (c t) -> p c t", c=2) for i in range(2)]
            R["tmp"] = [PF[:, 2048 + i * 512:2048 + (i + 1) * 512] for i in range(4)]
            R["stats"] = [PF[:, 4096 + i * 16:4096 + (i + 1) * 16] for i in range(4)]
            B = {}
            B["m"] = [S.pbuf(f"m{i}") for i in range(32)]
            for k in ("qT", "kT", "kz", "v", "sg"):
                B[k] = S.pbuf(k)
            for k, n in (("on", 4), ("stage", 2), ("tmp", 4), ("stats", 4), ("Sbf", 4)):
                B[k] = [S.pbuf(f"{k}{i}") for i in range(n)]
            B["PT"] = {pr: S.pbuf(f"PT{pr}") for pr in pairs}
            return R, B

        def rotary(R, B, view, Bv, n0, dst, Bdst, sidx):
            stg, Bstg = R["stage"][sidx], B["stage"][sidx]
            for j in range(2):
                bk, Bb = group_fm(view, Bv, n0 + j, 16, hT_rhs, BhT)
                S.op("scalar", lambda e, j=j, bk=bk: e.activation(out=stg[:, j, :], in_=bk[:], func=AF.Copy), reads=[Bb], writes=[Bstg])
            tm, Bt = R["tmp"], B["tmp"]
            cos, sin = cst[:, 0, :], cst[:, 1, :]
            tt = lambda o, a, b_, op: (lambda e: e.tensor_tensor(out=o, in0=a, in1=b_, op=op))
            S.op("vector", tt(tm[0], stg[:, 0, :], cos, ALU.mult), reads=[Bstg, Bcst], writes=[Bt[0]])
            S.op("vector", tt(tm[1], stg[:, 1, :], sin, ALU.mult), reads=[Bstg, Bcst], writes=[Bt[1]])
            S.op("vector", tt(tm[2], stg[:, 1, :], cos, ALU.mult), reads=[Bstg, Bcst], writes=[Bt[2]])
            S.op("vector", tt(tm[3], stg[:, 0, :], sin, ALU.mult), reads=[Bstg, Bcst], writes=[Bt[3]])
            S.op("vector", tt(dst[:, 0, :], tm[0], tm[1], ALU.subtract), reads=[Bt[0], Bt[1]], writes=[Bdst])
            S.op("vector", tt(dst[:, 1, :], tm[2], tm[3], ALU.add), reads=[Bt[2], Bt[3]], writes=[Bdst])

        def k_tokmajor(R, B, h):
            for blk in range(4):
                bk, Bb = nb()
                bkb = bk[:].bitcast(BF16)
                def fn(e, blk=blk, bkb=bkb):
                    for dc in range(2):
                        inst = e.transpose(out=bkb[:, dc * 128:(dc + 1) * 128], in_=R["kT"][:, dc, blk * 128:(blk + 1) * 128], identity=identb[:])
                    return inst
                S.op("tensor", fn, reads=[B["kT"], Bconst], writes=[Bb])
                S.op("vector", lambda e, blk=blk, bkb=bkb: e.tensor_scalar(out=R["kz"][:, blk, :], in0=bkb[:, 0:256], scalar1=kzs4(h, blk), scalar2=None, op0=ALU.mult),
                     reads=[Bb, Bconst], writes=[B["kz"]])

        def v_proj(R, B, h, pre=None):
            view, Bv = pre if pre is not None else load_panel(w_in, 16, [(OVV + h * DV, OVV + (h + 1) * DV)])
            for blk in range(4):
                bk, Bb = group_tm(view, Bv, blk, 16, 512, hT_lhs, BhT)
                copy_op(ev_eng(), R["v"][:, blk, :], bk[:], [Bb], [B["v"]])

        def state_update_tile(R, B, h):
            for dc in range(2):
                bk, Bb = nb()
                def fn(e, dc=dc, bk=bk):
                    for blk in range(4):
                        inst = e.matmul(bk[:], lhsT=R["kz"][:, blk, dc * 128:(dc + 1) * 128], rhs=R["v"][:, blk, :], start=(blk == 0), stop=(blk == 3))
                    return inst
                S.op("tensor", fn, reads=[B["kz"], B["v"]], writes=[Bb])
                i = h * 2 + dc
                S.op("vector", lambda e, i=i, bk=bk: e.scalar_tensor_tensor(out=S32[:, i, :], in0=S32[:, i, :], scalar=float(gam128[h] ** 4), in1=bk[:], op0=ALU.mult, op1=ALU.add),
                     reads=[Bb], writes=[BS32[i]])

        pending = []

        def ret_head(R, B, h):
            view, Bv = load_panel(w_in, 16, [(OQ + h * DK, OQ + (h + 1) * DK)])
            i_kv = R["t"] * H + h
            S.dma("sync", lambda e: e.dma_start(out=PB[:, 17408:21504], in_=kvs_d[i_kv]), "dkv",
                  reads=[Bkvs[i_kv]], writes=[B["kT"], B["kz"], B["v"]])
            rotary(R, B, view, Bv, 0, R["qT"], B["qT"], 0)
            while pending:
                pending.pop(0)()
            view, Bv = load_panel(w_in, 16, [(OG + h * DV, OG + (h + 1) * DV)])
            for n in range(4):
                bk, Bb = group_fm(view, Bv, n, 16, hT_rhs, BhT)
                S.op("scalar", lambda e, n=n, bk=bk: e.activation(out=R["sg"][:, n, :], in_=bk[:], func=AF.Silu), reads=[Bb], writes=[B["sg"]])

            for blk in range(4):
                for dc in range(2):
                    S.op("scalar", lambda e, dc=dc, blk=blk: e.activation(out=R["Sbf"][blk][:, dc, :], in_=S32[:, h * 2 + dc, :], func=AF.Identity, scale=float(gam128[h] ** blk)),
                         reads=[BS32[h * 2 + dc]], writes=[B["Sbf"][blk]])
            for kb in range(4):
                nq = 4 - kb
                bks, Bbs = nb()
                def fsc(e, bks=bks, kb=kb, nq=nq):
                    for dc in range(2):
                        inst = e.matmul(bks[:, 0:nq * 128], lhsT=R["kT"][:, dc, kb * 128:(kb + 1) * 128], rhs=R["qT"][:, dc, kb * 128:512], start=(dc == 0), stop=(dc == 1))
                    return inst
                S.op("tensor", fsc, reads=[B["kT"], B["qT"]], writes=[Bbs])
                for qb in range(kb, 4):
                    d = qb - kb
                    pt, Bpt = R["PT"][(kb, qb)], B["PT"][(kb, qb)]
                    if d == 0:
                        S.op("vector", lambda e, bks=bks, pt=pt: e.tensor_tensor(out=pt, in0=bks[:, 0:128], in1=mask[:, h, :], op=ALU.mult),
                             reads=[Bbs, Bconst], writes=[Bpt])
                    else:
                        S.op("vector", lambda e, bks=bks, pt=pt, d=d: e.tensor_scalar(out=pt, in0=bks[:, d * 128:(d + 1) * 128], scalar1=offc(h, d), scalar2=None, op0=ALU.mult),
                             reads=[Bbs, Bconst], writes=[Bpt])
            state_update_tile(R, B, h)

            def fo_stage(blk):
                tsl = slice(blk * 128, (blk + 1) * 128)
                Sb, BSb = R["Sbf"][blk], B["Sbf"][blk]
                bko, Bbo = nb()
                def fo(e, bko=bko, tsl=tsl, blk=blk, Sb=Sb):
                    for kb in range(blk + 1):
                        e.matmul(bko[:], lhsT=R["PT"][(kb, blk)], rhs=R["v"][:, kb, :], start=(kb == 0), stop=False)
                    e.matmul(bko[:], lhsT=R["qT"][:, 0, tsl], rhs=Sb[:, 0, :], start=False, stop=False)
                    return e.matmul(bko[:], lhsT=R["qT"][:, 1, tsl], rhs=Sb[:, 1, :], start=False, stop=True)
                S.op("tensor", fo, reads=[B["PT"][(kb, blk)] for kb in range(blk + 1)] + [B["v"], B["qT"], BSb], writes=[Bbo])
                sts, Bst_ = R["stats"][blk], B["stats"][blk]
                S.op("vector", lambda e, bko=bko, sts=sts: e.bn_stats(out=sts[:, 0:6], in_=bko[:]), reads=[Bbo], writes=[Bst_])
                S.op("vector", lambda e, sts=sts: e.bn_aggr(out=sts[:, 8:10], in_=sts[:, 0:6]), reads=[Bst_], writes=[Bst_])
                S.op("scalar", lambda e, sts=sts: e.activation(out=sts[:, 10:11], in_=sts[:, 9:10], func=AF.Sqrt, bias=epsx(h), scale=1.0),
                     reads=[Bst_, Bconst], writes=[Bst_])
                S.op("vector", lambda e, sts=sts: e.reciprocal(out=sts[:, 10:11], in_=sts[:, 10:11]), reads=[Bst_], writes=[Bst_])
                S.op("vector", lambda e, bko=bko, sts=sts, blk=blk: e.tensor_scalar(out=R["on"][blk], in0=bko[:], scalar1=sts[:, 8:9], scalar2=sts[:, 10:11], op0=ALU.subtract, op1=ALU.mult),
                     reads=[Bbo, Bst_], writes=[B["on"][blk]])

                def tr_stage(blk=blk, tsl=tsl):
                    bkt, Bbt = nb()
                    bktb = bkt[:].bitcast(BF16)
                    def ftr(e, bktb=bktb):
                        for ec in range(4):
                            inst = e.transpose(out=bktb[:, ec * 128:(ec + 1) * 128], in_=R["on"][blk][:, ec * 128:(ec + 1) * 128], identity=identb[:])
                        return inst
                    S.op("tensor", ftr, reads=[B["on"][blk], Bconst], writes=[Bbt])
                    for ec in range(4):
                        j = h * 4 + ec
                        S.op("vector", lambda e, ec=ec, j=j, bktb=bktb: e.scalar_tensor_tensor(out=R["m"][:, j, tsl], in0=bktb[:, ec * 128:(ec + 1) * 128], scalar=gret(j), in1=R["sg"][:, ec, tsl], op0=ALU.mult, op1=ALU.mult),
                             reads=[Bbt, B["sg"], Bconst], writes=[B["m"][j]])
                pending.append(tr_stage)

            for blk in range(4):
                fo_stage(blk)

        Bkvs = [Buf(f"kvs{i}") for i in range(NT * H)]
        kvstore = []
        carry = {}

        def phaseA_tile(t):
            cur["ph"], cur["t"] = "A", t
            load_x_tile(x_d, t)
            load_cs(1, t)
            norm(0)
            S.new_phase()
            R, B = alloc_ret()
            R2 = dict(R)
            B2 = dict(B)
            R2["kT"] = PB[:, 0:1024].rearrange("p (c t) -> p c t", c=2)
            R2["kz"] = PB[:, 1024:2048].rearrange("p (b d) -> p b d", b=4)
            R2["v"] = PB[:, 2048:4096].rearrange("p (b e) -> p b e", b=4)
            for k in ("kT", "kz", "v"):
                B2[k] = S.pbuf(k + "_alt")
            regs = [PB[:, 17408:21504], PB[:, 0:4096]]
            for h in range(H):
                r = h % 2
                Rr, Br = (R, B) if r == 0 else (R2, B2)
                view, Bv = load_panel(w_in, 16, [(OK_ + h * DK, OK_ + (h + 1) * DK)])
                vpre = load_panel(w_in, 16, [(OVV + h * DV, OVV + (h + 1) * DV)])
                if t >= 1 and pre_list:
                    preconvert(*pre_list.pop(0))
                while kvstore:
                    kvstore.pop(0)()
                rotary(Rr, Br, view, Bv, 0, Rr["kT"], Br["kT"], 1)
                v_proj(Rr, Br, h, pre=vpre)
                k_tokmajor(Rr, Br, h)
                i = t * H + h
                def do_store(i=i, r=r, Br=Br):
                    S.dma("sync", lambda e: e.dma_start(out=kvs_d[i], in_=regs[r]), f"dks{r}",
                          reads=[Br["kT"], Br["kz"], Br["v"]], writes=[Bkvs[i]])
                kvstore.append(do_store)
                state_update_tile(Rr, Br, h)
                if t == NT - 1:
                    cst2 = PF[:, 4200:4232].rearrange("p (c t) -> p c t", c=16)
                    if h == 0:
                        carry["Bc2"] = S.pbuf("c2")
                    Bc2 = carry["Bc2"]
                    rhs2 = lambda kc: hT[:, kc, T - 2:T]
                    j = h % 4
                    if h < 4:
                        view, Bv = load_panel(w_in, 16, [(OC + j * 512, OC + (j + 1) * 512)])
                        for n in range(4):
                            bk, Bb = group_fm(view, Bv, n, 16, rhs2, BhT, N=2)
                            S.op("scalar", lambda e, c=j * 4 + n, bk=bk: e.activation(out=cst2[:, c, :], in_=bk[:, 0:2], func=AF.Copy), reads=[Bb], writes=[Bc2])
                    else:
                        view, Bv = load_panel(w_in, 16, [(OV + j * 512, OV + (j + 1) * 512)])
                        for n in range(4):
                            bk, Bb = group_fm(view, Bv, n, 16, rhs2, BhT, N=2)
                            S.op("vector", lambda e, c=j * 4 + n, bk=bk: e.tensor_tensor(out=ucarry[:, c, :], in0=bk[:, 0:2], in1=cst2[:, c, :], op=ALU.mult),
                                 reads=[Bb, Bc2], writes=[Bucar])
                if t == NT - 1 and h == 3:
                    exchange_send(0)
            while kvstore:
                kvstore.pop(0)()

        xparts = []
        groups = [[0, 1], [2, 3], [4, 5], [6, 7]]

        def exchange_send(i):
            tin, tout = xch[i]
            Bxin, Bxout = Buf(f"xin{i}"), Buf(f"xout{i}")
            if i < 2:
                src = S32[:, i * 8:(i + 1) * 8, :]
                rb = BS32[i * 8:(i + 1) * 8]
                din_ap = tin.ap().rearrange("(p a) b -> p a b", a=8)
                dout_ap = tout.ap()[0:1024, :].rearrange("(p a) b -> p a b", a=8)
            else:
                src = ucarry[:].rearrange("p a b -> p (a b)")
                rb = [Bucar]
                din_ap = tin.ap()
                dout_ap = tout.ap()[0:128, :]
            S.dma("sync", lambda e: e.dma_start(out=din_ap, in_=src), f"dxi{i}", reads=rb, writes=[Bxin])
            S.cc("gpsimd", lambda e: e.collective_compute("AllGather", ALU.bypass, replica_groups=groups,
                                                          ins=[tin.ap().opt()], outs=[tout.ap().opt()]),
                 f"cc{i}", reads=[Bxin], writes=[Bxout])
            xparts.append((i, Bxout, src, rb, dout_ap))

        def exchange_recv(i):
            _, Bxout, src, rb, dout_ap = [p for p in xparts if p[0] == i][0]
            S.dma("sync", lambda e: e.dma_start(out=src, in_=dout_ap), f"dxo{i}", reads=[Bxout], writes=rb)
            flag = small[:, 160:161]
            if i < 2:
                for k in range(i * 8, (i + 1) * 8):
                    S.op("vector", lambda e, k=k: e.tensor_scalar(out=S32[:, k, :], in0=S32[:, k, :], scalar1=flag, scalar2=None, op0=ALU.mult),
                         reads=[Bconst], writes=[BS32[k]])
            else:
                ucf = ucarry[:].rearrange("p a b -> p (a b)")
                S.op("vector", lambda e: e.tensor_scalar(out=ucf, in0=ucf, scalar1=flag, scalar2=None, op0=ALU.mult),
                     reads=[Bconst], writes=[Bucar])

        out_toks = []

        def main_tile(t):
            cur["ph"], cur["t"] = "M", t
            load_x_tile(x_d, t)
            load_cs(1, t)
            norm(0)
            S.new_phase()
            R, B = alloc_ret()
            R["t"] = t
            for h in range(H):
                if t == 0 and h == 4:
                    exchange_recv(1)
                    exchange_recv(2)
                ret_head(R, B, h)
            while pending:
                pending.pop(0)()
            S.new_phase(keep=B["m"])
            mg = PB[:, 18432:26624].rearrange("p (c t) -> p c t", c=16)
            sgr = [PB[:, 16384:18432].rearrange("p (c t) -> p c t", c=4)]
            Bsgr = [S.pbuf("sgr0")]
            Bmg = [S.pbuf(f"mg{i}") for i in range(16)]
            m_rhs = lambda kc: R["m"][:, kc, :]
            for j in range(4):
                view, Bv = load_panel(w_in, 16, [(OGR + j * 512, OGR + (j + 1) * 512)])
                for n in range(4):
                    bk, Bb = group_fm(view, Bv, n, 16, hT_rhs, BhT)
                    S.op("scalar", lambda e, n=n, bk=bk: e.activation(out=sgr[0][:, n, :], in_=bk[:], func=AF.Sigmoid), reads=[Bb], writes=[Bsgr[0]])
                def ev_ro(n, bk, Bb, j=j):
                    c = j * 4 + n
                    S.op("vector", lambda e, c=c, n=n, bk=bk: e.tensor_tensor(out=mg[:, c, :], in0=bk[:], in1=sgr[0][:, n, :], op=ALU.mult),
                         reads=[Bb, Bsgr[0]], writes=[Bmg[c]])
                proj_ksplit(w_ro, 32, 16, j * 512, m_rhs, B["m"], ev_ro)
            S.new_phase(keep=Bmg)
            z = PB[:, 0:8192].rearrange("p (c t) -> p c t", c=16)
            Bz = [S.pbuf(f"z{i}") for i in range(16)]
            sgc = PB[:, 8192:10240].rearrange("p (c t) -> p c t", c=4)
            Bsgc = S.pbuf("sgc")
            Bst = PF[:, 0:2048].rearrange("p (c t) -> p c t", c=4)
            BBst = S.pbuf("Bst")
            uh = PF[:, 2048:2048 + 4 * 514].rearrange("p (c t) -> p c t", c=4)
            Buh = [S.pbuf(f"uh{i}") for i in range(4)]
            ytmp = PF[:, 4104:4616]
            Byt = S.pbuf("ytmp")
            for j in range(4):
                view, Bv = load_panel(w_in, 16, [(OB + j * 512, OB + (j + 1) * 512)])
                for n in range(4):
                    bk, Bb = group_fm(view, Bv, n, 16, hT_rhs, BhT)
                    S.op("scalar", lambda e, n=n, bk=bk: e.activation(out=Bst[:, n, :], in_=bk[:], func=AF.Copy), reads=[Bb], writes=[BBst])
                view, Bv = load_panel(w_in, 16, [(OC + j * 512, OC + (j + 1) * 512)])
                for n in range(4):
                    bk, Bb = group_fm(view, Bv, n, 16, hT_rhs, BhT)
                    S.op("scalar", lambda e, n=n, bk=bk: e.activation(out=uh[:, n, 2:514], in_=bk[:], func=AF.Copy), reads=[Bb], writes=[Buh[n]])
                view, Bv = load_panel(w_in, 16, [(OV + j * 512, OV + (j + 1) * 512)])
                for n in range(4):
                    c = j * 4 + n
                    bk, Bb = group_fm(view, Bv, n, 16, hT_rhs, BhT)
                    S.op("vector", lambda e, n=n, bk=bk: e.tensor_tensor(out=uh[:, n, 2:514], in0=bk[:], in1=uh[:, n, 2:514], op=ALU.mult),
                         reads=[Bb], writes=[Buh[n]])
                    S.op("scalar", lambda e, n=n, c=c: e.activation(out=uh[:, n, 0:2], in_=ucarry[:, c, :], func=AF.Copy), reads=[Bucar], writes=[Buh[n]])
                    S.op("scalar", lambda e, n=n, c=c: e.activation(out=ytmp, in_=uh[:, n, 0:512], func=AF.Identity, scale=cwc(0, c)),
                         reads=[Buh[n], Bconst], writes=[Byt])
                    S.op("scalar", lambda e, n=n, c=c: e.activation(out=ucarry[:, c, :], in_=uh[:, n, 512:514], func=AF.Copy), reads=[Buh[n]], writes=[Bucar])
                    S.op("vector", lambda e, n=n, c=c: e.scalar_tensor_tensor(out=ytmp, in0=uh[:, n, 1:513], scalar=cwc(1, c), in1=ytmp, op0=ALU.mult, op1=ALU.add),
                         reads=[Buh[n], Bconst], writes=[Byt])
                    S.op("vector", lambda e, n=n, c=c: e.scalar_tensor_tensor(out=ytmp, in0=uh[:, n, 2:514], scalar=cwc(2, c), in1=ytmp, op0=ALU.mult, op1=ALU.add),
                         reads=[Buh[n], Bconst], writes=[Byt])
                    S.op("vector", lambda e, n=n, c=c: e.tensor_tensor(out=z[:, c, :], in0=ytmp, in1=Bst[:, n, :], op=ALU.mult),
                         reads=[Byt, BBst], writes=[Bz[c]])
            z_rhs = lambda kc: z[:, kc, :]
            tmpf = ytmp
            Btmpf = Byt
            for j in range(4):
                view, Bv = load_panel(w_in, 16, [(OGC + j * 512, OGC + (j + 1) * 512)])
                for n in range(4):
                    bk, Bb = group_fm(view, Bv, n, 16, hT_rhs, BhT)
                    S.op("scalar", lambda e, n=n, bk=bk: e.activation(out=sgc[:, n, :], in_=bk[:], func=AF.Sigmoid), reads=[Bb], writes=[Bsgc])
                view, Bv = load_panel(w_co, 16, [(j * 512, (j + 1) * 512)])
                for n in range(4):
                    c = j * 4 + n
                    bk, Bb = group_fm(view, Bv, n, 16, z_rhs, Bz)
                    S.op("vector", lambda e, n=n, bk=bk: e.tensor_tensor(out=tmpf, in0=bk[:], in1=sgc[:, n, :], op=ALU.mult),
                         reads=[Bb, Bsgc], writes=[Btmpf])
                    S.op("vector", lambda e, c=c: e.tensor_tensor(out=mg[:, c, :], in0=tmpf, in1=mg[:, c, :], op=ALU.add),
                         reads=[Btmpf], writes=[Bmg[c]])
            mg_rhs = lambda kc: mg[:, kc, :]
            for j in range(4):
                view, Bv = load_panel(w_o, 16, [(j * 512, (j + 1) * 512)])
                for n in range(4):
                    c = j * 4 + n
                    bk, Bb = group_fm(view, Bv, n, 16, mg_rhs, Bmg)
                    S.op("vector", lambda e, c=c, bk=bk: e.tensor_tensor(out=xT[:, c, :], in0=bk[:], in1=xT[:, c, :], op=ALU.add),
                         reads=[Bb], writes=[BxT[c]])
                    stats_chunk(c)
            norm(1)
            S.new_phase()
            act = PB[:, 0:22528].rearrange("p (c t) -> p c t", c=44)
            Bact = [S.pbuf(f"act{i}") for i in range(44)]
            sa = PF[:, 0:2048].rearrange("p (c t) -> p c t", c=4)
            Bsa = S.pbuf("sa")
            for j in range(11):
                view, Bv = load_panel(w_fi, 16, [(j * 512, (j + 1) * 512)])
                for n in range(4):
                    bk, Bb = group_fm(view, Bv, n, 16, hT_rhs, BhT)
                    S.op("scalar", lambda e, n=n, bk=bk: e.activation(out=sa[:, n, :], in_=bk[:], func=AF.Silu), reads=[Bb], writes=[Bsa])
                view, Bv = load_panel(w_fi, 16, [(DFF + j * 512, DFF + (j + 1) * 512)])
                for n in range(4):
                    c = j * 4 + n
                    bk, Bb = group_fm(view, Bv, n, 16, hT_rhs, BhT)
                    S.op("vector", lambda e, n=n, c=c, bk=bk: e.tensor_tensor(out=act[:, c, :], in0=bk[:], in1=sa[:, n, :], op=ALU.mult),
                         reads=[Bb, Bsa], writes=[Bact[c]])
            act_rhs = lambda kc: act[:, kc, :]
            for j in range(4):
                def ev_fo(n, bk, Bb, j=j):
                    c = j * 4 + n
                    S.op("vector", lambda e, c=c, bk=bk: e.tensor_tensor(out=xT[:, c, :], in0=bk[:], in1=xT[:, c, :], op=ALU.add),
                         reads=[Bb], writes=[BxT[c]])
                    stats_chunk(c)
                proj_ksplit(w_fo, 44, 11, j * 512, act_rhs, Bact, ev_fo)
            norm(2)
            S.new_phase()
            pT = PB[:, 0:1024].rearrange("p (c t) -> p c t", c=2)
            BpT = S.pbuf("pT")
            sgp = PF[:, 0:2048].rearrange("p (c t) -> p c t", c=4)
            Bsgp = S.pbuf("sgp")
            tmp2 = PF[:, 2048:2560]
            Btmp2 = S.pbuf("tmp2")
            pst = PF[:, 2560:3584].rearrange("p (b f) -> p b f", b=4)
            Bpst = S.pbuf("pst")
            S.dma("sync", lambda e: e.dma_start(out=pst, in_=p_d[t * T:(t + 1) * T, :].rearrange("(b p) f -> p b f", p=128)), "dp", writes=[Bpst])
            for blk in range(4):
                bk, Bb = nb()
                def fpt(e, blk=blk, bk=bk):
                    for fc in range(2):
                        inst = e.transpose(out=bk[:, fc * 128:(fc + 1) * 128], in_=pst[:, blk, fc * 128:(fc + 1) * 128], identity=identf[:])
                    return inst
                S.op("tensor", fpt, reads=[Bpst, Bconst], writes=[Bb])
                copy_op(ev_eng(), pT[:, :, blk * 128:(blk + 1) * 128], bk[:, 0:256].rearrange("p (c t) -> p c t", c=2), [Bb], [BpT])
            vpp = PB[:, 1024:5120].rearrange("p (kc n) -> p kc n", kc=2)
            Bvpp = S.pbuf("wpp")
            S.dma("gpsimd", lambda e: e.dma_start(out=vpp, in_=w_pp.rearrange("(kc p) n -> p kc n", p=128)), "dwp", writes=[Bvpp])
            ppf = PB[:, 5120:21504].bitcast(F32).rearrange("p (c t) -> p c t", c=16)
            Bppf = [S.pbuf(f"ppf{i}") for i in range(16)]
            for c in range(16):
                bk, Bb = group_fm(vpp, Bvpp, c, 2, lambda kc: pT[:, kc, :], [BpT])
                copy_op(ev_eng(), ppf[:, c, :], bk[:], [Bb], [Bppf[c]])
            sgp1 = [PF[:, i * 512:(i + 1) * 512] for i in range(4)]
            Bsgp1 = [S.pbuf(f"sgp{i}") for i in range(4)]
            for j in range(4):
                view, Bv = load_panel(w_pg, 16, [(j * 512, (j + 1) * 512)])
                for n in range(4):
                    c = j * 4 + n
                    bk, Bb = group_fm(view, Bv, n, 16, hT_rhs, BhT)
                    S.op("scalar", lambda e, n=n, bk=bk: e.activation(out=sgp1[n], in_=bk[:], func=AF.Sigmoid), reads=[Bb], writes=[Bsgp1[n]])
                    S.op("vector", lambda e, n=n, c=c: e.tensor_tensor(out=sgp1[n], in0=sgp1[n], in1=ppf[:, c, :], op=ALU.mult),
                         reads=[Bppf[c]], writes=[Bsgp1[n]])
                    S.op("vector", lambda e, n=n, c=c: e.tensor_tensor(out=xT[:, c, :], in0=sgp1[n], in1=xT[:, c, :], op=ALU.add),
                         reads=[Bsgp1[n]], writes=[BxT[c]])
                    stats_chunk(c)
            norm(3, inplace=True)
            S.new_phase()
            ost = [PF[:, i * 2048:(i + 1) * 2048] for i in range(2)]
            Bost = [S.pbuf(f"ost{i}") for i in range(2)]
            for blk in range(4):
                r = blk % 2
                for g in range(4):
                    bk, Bb = nb()
                    def fo2(e, g=g, blk=blk, bk=bk):
                        for j in range(4):
                            kc = g * 4 + j
                            inst = e.transpose(out=bk[:, j * 128:(j + 1) * 128], in_=xT[:, kc, blk * 128:(blk + 1) * 128], identity=identf[:])
                        return inst
                    S.op("tensor", fo2, reads=BxT[g * 4:(g + 1) * 4] + [Bconst], writes=[Bb])
                    copy_op(ev_eng(), ost[r][:, g * 512:(g + 1) * 512], bk[:], [Bb], [Bost[r]])
                r0 = t * T + blk * 128
                tok = S.dma("sync", lambda e, r=r, r0=r0: e.dma_start(out=out_d[r0:r0 + 128, :], in_=ost[r]), f"do{r}", reads=[Bost[r]])
                out_toks.append(tok)

        for t in range(NT):
            phaseA_tile(t)
        exchange_send(1)
        exchange_send(2)
        exchange_recv(0)
        for t in range(NT):
            main_tile(t)
        S.wait_all("sync", out_toks)
        build_nc.info = dict(npan=len(scr_ids), counts={k: len(v) for k, v in S.prog.items()})
        S.emit(block)
    return nc


_CACHE = {}


def _consts():
    hh = np.arange(H, dtype=np.float64)
    lg = np.log1p(-np.exp2(-5.0 - hh))
    idx = np.arange(128, dtype=np.float64)
    c = idx[None, :]
    m = idx[:, None]
    same = (c // 64) == (m // 64)
    earlier = (m // 64) < (c // 64)
    maskT = np.zeros((128, H, 128), np.float64)
    for h in range(H):
        e_same = np.abs(c - m) - c - 1.0
        e_ear = -m - 1.0 + 0.0 * c
        val = np.where(same, np.exp(lg[h] * e_same), np.where(earlier, np.exp(lg[h] * e_ear), 0.0))
        maskT[:, h, :] = val * (DK ** -0.5)
    kzs4 = np.zeros((128, H, 4), np.float64)
    for b in range(4):
        kzs4[:, :, b] = np.exp(lg[None, :] * (127.0 - idx[:, None] + 128.0 * (3 - b))) * (DK ** -0.5)
    offc = np.zeros((128, H, 3), np.float64)
    for d in range(1, 4):
        offc[:, :, d - 1] = np.exp(lg[None, :] * (128.0 * d - idx[:, None] - 1.0)) * (DK ** -0.5)
    xi = np.exp(lg[None, :] * (idx[:, None] + 1.0))
    epsx = EPS / (xi * xi)
    gam128 = np.exp(lg * 128.0)
    return maskT.astype(np.float32), kzs4.reshape(128, H * 4).astype(np.float32), offc.reshape(128, H * 3).astype(np.float32), epsx.astype(np.float32), gam128


def _cs_table(pos):
    half = 128
    inv = (10000.0 ** (-np.arange(half, dtype=np.float32) / half)).astype(np.float32)
    ang = pos.astype(np.float32)[None, :] * inv[:, None]
    return np.cos(ang).astype(np.float32), np.sin(ang).astype(np.float32)


def kernel(x, p, g_mix, w_in, conv_w, w_conv_out, g_ret, w_ret_out, w_o,
           g_ffn, w_ffn_in, w_ffn_out, g_ple, w_ple_gate, w_ple_proj, g_final):
    f = lambda a: np.ascontiguousarray(np.asarray(a, dtype=np.float32))
    x = f(x); p = f(p)
    maskT, kzs4, offc, epsx, gam128 = _consts()
    if "nc" not in _CACHE:
        _CACHE["nc"] = build_nc(gam128)
    nc = _CACHE["nc"]
    small = np.zeros((128, 256), np.float32)
    for i, g in enumerate((g_mix[0], g_ffn[0], g_ple[0], g_final)):
        small[:, i * 16:(i + 1) * 16] = f(g).reshape(16, 128).T
    small[:, 64:96] = f(g_ret[0]).reshape(32, 128).T
    cw = f(conv_w[0])
    for tap in range(3):
        small[:, 96 + tap * 16:96 + (tap + 1) * 16] = cw[tap].reshape(16, 128).T
    small[:, 161:193] = kzs4
    small[:, 193:217] = offc
    small[:, 152:160] = epsx
    ident = np.eye(128, dtype=np.float32)
    shared = {
        "w_in": f(w_in[0]), "w_conv_out": f(w_conv_out[0]), "w_ret_out": f(w_ret_out[0]), "w_o": f(w_o[0]),
        "w_ffn_in": f(w_ffn_in[0]), "w_ffn_out": f(w_ffn_out[0]), "w_ple_gate": f(w_ple_gate[0]),
        "w_ple_proj": f(w_ple_proj[0]), "maskT": np.ascontiguousarray(maskT.reshape(128, H * 128)),
        "small": small, "ident": ident,
    }
    cs_lo = _cs_table(np.arange(0, TOK))
    cs_hi = _cs_table(np.arange(TOK, 2 * TOK))
    in_maps = []
    for c in range(8):
        b, half = c // 2, c % 2
        m = dict(shared)
        m["x"] = np.ascontiguousarray(x[b, half * TOK:(half + 1) * TOK])
        m["p"] = np.ascontiguousarray(p[0, b, half * TOK:(half + 1) * TOK])
        cm = cs_hi if half == 1 else cs_lo
        m["cs"] = np.ascontiguousarray(np.stack([cm[0], cm[1], cm[0], cm[1]]))
        sm = small.copy()
        sm[:, 160] = float(half)
        m["small"] = sm
        in_maps.append(m)
    res = run_bass_kernel_spmd(nc, in_maps, core_ids=list(range(8)))
    out = np.empty((4, 2 * TOK, D), np.float32)
    for c in range(8):
        b, half = c // 2, c % 2
        out[b, half * TOK:(half + 1) * TOK] = res.results[c]["out"]
    return out
```

```python
import numpy as np
import ml_dtypes
from contextlib import ExitStack
import concourse.bass as bass
import concourse.mybir as mybir
from concourse.bass_utils import run_bass_kernel_spmd

F32 = mybir.dt.float32
BF16 = mybir.dt.bfloat16
AF = mybir.ActivationFunctionType
ALU = mybir.AluOpType

D = 2048
T = 512
NT = 4
TOK = 2048
H = 8
DK = 256
DV = 512
DFF = 5632
PLE = 256
NIN = 22528
EPS = 1e-6
OB, OC, OV, OQ, OK_, OVV, OG, OGC, OGR = 0, 2048, 4096, 6144, 8192, 10240, 14336, 18432, 20480
NSLOT = 2
NBANK = 8
NPAN = 112


class Buf:
    __slots__ = ("name", "w", "r", "const")

    def __init__(self, name, const=False, gate=()):
        self.name = name
        self.w = None
        self.r = list(gate)
        self.const = const


class Sched:
    ENGS = ("sync", "scalar", "vector", "gpsimd", "tensor")

    def __init__(self):
        self.prog = {e: [] for e in self.ENGS}
        self.sems = {}
        self.count = {}
        self.waited = {e: {} for e in self.ENGS}
        self.phase_bufs = []
        self.gate = []

    def add_sem(self, key, handle):
        self.sems[key] = handle
        self.count[key] = 0

    def _deps(self, eng, reads, writes, extra=()):
        deps = {}

        def add(tok):
            if tok is None:
                return
            k, v = tok
            if deps.get(k, 0) < v:
                deps[k] = v
        for b in reads:
            add(b.w)
        for b in writes:
            add(b.w)
            for t in b.r:
                add(t)
        for t in extra:
            add(t)
        waits = []
        wd = self.waited[eng]
        for k, v in deps.items():
            if wd.get(k, 0) < v:
                wd[k] = v
                waits.append((k, v))
        return waits

    def _mark(self, tok, reads, writes):
        for b in reads:
            if not b.const:
                b.r.append(tok)
        for b in writes:
            b.w = tok
            b.r = []

    def op(self, eng, fn, reads=(), writes=(), extra=()):
        waits = self._deps(eng, reads, writes, extra)
        self.count[eng] += 1
        tok = (eng, self.count[eng])
        self.prog[eng].append((waits, [fn], (eng, 1)))
        self._mark(tok, reads, writes)
        return tok

    def dma(self, eng, fns, semkey, reads=(), writes=(), extra=()):
        waits = self._deps(eng, reads, writes, extra)
        if not isinstance(fns, (list, tuple)):
            fns = [fns]
        self.count[semkey] += 16 * len(fns)
        tok = (semkey, self.count[semkey])
        self.prog[eng].append((waits, list(fns), (semkey, 16)))
        self._mark(tok, reads, writes)
        return tok

    def cc(self, eng, fn, semkey, reads=(), writes=()):
        waits = self._deps(eng, reads, writes)
        self.count[semkey] += 1
        tok = (semkey, self.count[semkey])
        self.prog[eng].append((waits, [fn], (semkey, 1)))
        self._mark(tok, reads, writes)
        return tok

    def wait_all(self, eng, toks):
        waits = self._deps(eng, (), (), toks)
        self.prog[eng].append((waits, [], None))

    def pbuf(self, name):
        b = Buf(name, gate=self.gate)
        self.phase_bufs.append(b)
        return b

    def new_phase(self, keep=()):
        mx = {}
        for b in self.phase_bufs:
            toks = list(b.r)
            if b.w is not None:
                toks.append(b.w)
            for k, v in toks:
                if mx.get(k, 0) < v:
                    mx[k] = v
        for k, v in self.gate:
            if mx.get(k, 0) < v:
                mx[k] = v
        self.gate = list(mx.items())
        self.phase_bufs = list(keep)

    def emit(self, block):
        S = self

        def mk(ename):
            def body(e):
                for waits, fns, inc in S.prog[ename]:
                    for k, v in waits:
                        e.wait_ge(S.sems[k], v)
                    for fn in fns:
                        inst = fn(e)
                        if inc is not None and (inc[1] == 16):
                            inst.then_inc(S.sems[inc[0]], 16)
                    if inc is not None and inc[1] == 1 and fns:
                        inst.then_inc(S.sems[inc[0]], 1)
            return body
        block.sync(mk("sync"))
        block.scalar(mk("scalar"))
        block.vector(mk("vector"))
        block.gpsimd(mk("gpsimd"))
        block.tensor(mk("tensor"))


def build_nc(gam128):
    nc = bass.Bass("TRN2", target_bir_lowering=False)

    def din(name, shape):
        return nc.dram_tensor(name, list(shape), F32, kind="ExternalInput").ap()
    x_d = din("x", [TOK, D])
    p_d = din("p", [TOK, PLE])
    w_in = din("w_in", [D, NIN])
    w_co = din("w_conv_out", [D, D])
    w_ro = din("w_ret_out", [H * DV, D])
    w_o = din("w_o", [D, D])
    w_fi = din("w_ffn_in", [D, 2 * DFF])
    w_fo = din("w_ffn_out", [DFF, D])
    w_pg = din("w_ple_gate", [D, D])
    w_pp = din("w_ple_proj", [PLE, D])
    cs_d = din("cs", [4, 128, TOK])
    mask_d = din("maskT", [128, H * 128])
    small_d = din("small", [128, 256])
    ident_d = din("ident", [128, 128])
    out_d = nc.dram_tensor("out", [TOK, D], F32, kind="ExternalOutput").ap()
    scr_d = nc.dram_tensor("scr", [NPAN, 128, 8192], BF16, kind="Internal").ap()
    kvs_d = nc.dram_tensor("kvs", [NT * H, 128, 4096], BF16, kind="Internal").ap()
    xch = [(nc.dram_tensor(f"xin{i}", [r, w], F32), nc.dram_tensor(f"xout{i}", [2 * r, w], F32)) for i, (r, w) in enumerate(((1024, 512), (1024, 512), (128, 32)))]

    with ExitStack() as es:
        def sb(name, shape, dt):
            return es.enter_context(nc.sbuf_tensor("sb_" + name, list(shape), dt))
        xT = sb("xT", [128, 16, T], F32)
        hT = sb("hT", [128, 16, T], BF16)
        wsl = [sb(f"wsl{i}", [128, 8192], BF16) for i in range(NSLOT)]
        S32 = sb("S32", [128, 16, 512], F32)
        cst = sb("cst", [128, 2, T], F32)
        mask = sb("mask", [128, H, 128], F32)
        small = sb("small", [128, 256], F32)
        identf = sb("identf", [128, 128], F32)
        identb = sb("identb", [128, 128], BF16)
        ones = sb("ones", [128, 128], BF16)
        sq = sb("sq", [128, 4, T], BF16)
        rstd = sb("rstd", [128, T], F32)
        ucarry = sb("ucarry", [128, 16, 2], F32)
        epsc = sb("epsc", [128, 2], F32)
        PB = sb("PB", [128, 30976], BF16)
        PF = sb("PF", [128, 4640], F32)
        banks = [es.enter_context(nc.psum_tensor(f"psbank{i}", [128, 512], F32)) for i in range(NBANK)]

        S = Sched()
        semnames = list(Sched.ENGS) + [f"dw{i}" for i in range(NSLOT)] + [f"dsw{i}" for i in range(NSLOT)] + ["dkv", "dks0", "dks1", "dxi0", "dxi1", "dxi2","dpc0", "dpc1", "dpc2", "dpc3", "dpc4", "dpc5", "dpc6", "dpc7", "dpc8", "dpc9", "dpc10", "dpc11", "dpc12", "dpc13", "dpc14", "dpc15", "dxo0", "dxo1", "dxo2", "cc0", "cc1", "cc2", "dx0", "dx1", "dc", "dcs", "dp", "do0", "do1", "dwp"]
        for k in semnames:
            S.add_sem(k, es.enter_context(nc.semaphore("sem_" + k)))
        block = es.enter_context(nc.Block())

        BxT = [Buf(f"xT{k}") for k in range(16)]
        BhT = [Buf(f"hT{k}") for k in range(16)]
        Bw = [Buf(f"w{i}") for i in range(NSLOT)]
        BS32 = [Buf(f"S32_{i}") for i in range(16)]
        Bcst = Buf("cst")
        Bconst = Buf("const", const=True)
        Bsq = [Buf(f"sq{i}") for i in range(4)]
        Brstd = Buf("rstd")
        Bucar = Buf("ucarry")
        Bbank = [Buf(f"bank{i}") for i in range(NBANK)]
        st = {"bank": 0, "slot": 0, "ev": 0}

        gcol = lambda which, kc: small[:, which * 16 + kc: which * 16 + kc + 1]
        gret = lambda j: small[:, 64 + j: 65 + j]
        cwc = lambda tap, kc: small[:, 96 + tap * 16 + kc: 97 + tap * 16 + kc]
        kzs4 = lambda h, b: small[:, 161 + h * 4 + b: 162 + h * 4 + b]
        offc = lambda h, d: small[:, 193 + h * 3 + d - 1: 194 + h * 3 + d - 1]
        epsx = lambda h: small[:, 152 + h: 153 + h]

        def nb():
            i = st["bank"]
            st["bank"] = (i + 1) % (NBANK - 1)
            return banks[i], Bbank[i]

        def ev_eng():
            st["ev"] ^= 1
            return "scalar" if st["ev"] else "vector"

        def copy_op(eng, out, in_, reads, writes):
            if eng == "scalar":
                return S.op("scalar", lambda e: e.activation(out=out, in_=in_, func=AF.Copy), reads=reads, writes=writes)
            return S.op("vector", lambda e: e.tensor_copy(out=out, in_=in_), reads=reads, writes=writes)

        S.dma("sync", [lambda e: e.dma_start(out=mask[:].rearrange("p h c -> p (h c)"), in_=mask_d),
                       lambda e: e.dma_start(out=small[:], in_=small_d),
                       lambda e: e.dma_start(out=identf[:], in_=ident_d)], "dc", writes=[Bconst])
        S.op("vector", lambda e: e.tensor_copy(out=identb[:], in_=identf[:]), reads=[Bconst], writes=[Bconst])
        S.op("vector", lambda e: e.memset(ones[:], 1.0 / D), writes=[Bconst])
        S.op("vector", lambda e: e.memset(epsc[:], EPS), writes=[Bconst])
        S.op("vector", lambda e: e.memset(S32[:].rearrange("p a b -> p (a b)"), 0.0), writes=BS32)
        S.op("vector", lambda e: e.memset(ucarry[:].rearrange("p a b -> p (a b)"), 0.0), writes=[Bucar])

        scr_ids = {}
        Bscr = {}
        scr_ok = {}
        wb_at = {}
        cur = {"ph": "A", "t": 0, "nmain": 0}

        def load_panel(wap, KC, colranges, rowq=0):
            s = st["slot"]
            st["slot"] = (s + 1) % NSLOT
            ncols = sum(c1 - c0 for c0, c1 in colranges)
            assert KC * ncols <= 8192
            flat = wsl[s][:, 0:KC * ncols]
            view = flat.rearrange("p (kc n) -> p kc n", kc=KC)
            key = (wap.name, rowq, KC, tuple(colranges))
            now = (cur["ph"], cur["t"])
            if key in scr_ids and scr_ok.get(key):
                pid = scr_ids[key]
                S.dma("sync", lambda e: e.dma_start(out=flat, in_=scr_d[pid, :, 0:KC * ncols]), f"dw{s}", reads=[Bscr[pid]], writes=[Bw[s]])
                return view, Bw[s]
            wv = wap.rearrange("(kc p) n -> p kc n", p=128)
            fns = []
            off = 0
            for c0, c1 in colranges:
                n = c1 - c0
                fns.append(lambda e, off=off, n=n, c0=c0, c1=c1: e.dma_start(out=view[:, :, off:off + n], in_=wv[:, :, c0:c1]))
                off += n
            S.dma("gpsimd", fns, f"dw{s}", writes=[Bw[s]])
            if key not in scr_ids:
                pid = len(scr_ids)
                assert pid < NPAN
                scr_ids[key] = pid
                Bscr[pid] = Buf(f"scr{pid}")
                tgt = now
                wb_at[key] = tgt
            if wb_at[key] == now:
                pid = scr_ids[key]
                S.dma("sync", lambda e: e.dma_start(out=scr_d[pid, :, 0:KC * ncols], in_=flat), f"dsw{s}", reads=[Bw[s]], writes=[Bscr[pid]])
                scr_ok[key] = True
            return view, Bw[s]

        npre = [0]

        def preconvert(wap, KC, colranges, rowq=0):
            key = (wap.name, rowq, KC, tuple(colranges))
            if key in scr_ids:
                return
            ncols = sum(c1 - c0 for c0, c1 in colranges)
            pid = len(scr_ids)
            assert pid < NPAN
            scr_ids[key] = pid
            Bscr[pid] = Buf(f"scr{pid}")
            wv = wap.rearrange("(kc p) n -> p kc n", p=128)
            dst = scr_d[pid, :, 0:KC * ncols].rearrange("p (kc n) -> p kc n", kc=KC)
            fns = []
            off = 0
            for c0, c1 in colranges:
                n = c1 - c0
                fns.append(lambda e, off=off, n=n, c0=c0, c1=c1: e.dma_start(out=dst[:, :, off:off + n], in_=wv[:, :, c0:c1]))
                off += n
            S.dma("gpsimd", fns, f"dpc{npre[0]}", writes=[Bscr[pid]])
            npre[0] += 1
            scr_ok[key] = True
            wb_at[key] = None

        pre_list = ([(w_in, 16, [(OC + j * 512, OC + (j + 1) * 512)]) for j in range(4)]
                    + [(w_in, 16, [(OV + j * 512, OV + (j + 1) * 512)]) for j in range(4)])

        def group_fm(view, Bv, n, KC, rhs_fn, rbufs, N=T):
            bk, Bb = nb()
            def fn(e):
                for kc in range(KC):
                    inst = e.matmul(bk[:, 0:N], lhsT=view[:, kc, n * 128:(n + 1) * 128], rhs=rhs_fn(kc),
                                    start=(kc == 0), stop=(kc == KC - 1))
                return inst
            S.op("tensor", fn, reads=[Bv] + list(rbufs), writes=[Bb])
            flush_stats()
            return bk, Bb

        def group_tm(view, Bv, blk, KC, ncols, lhs_fn, rbufs):
            bk, Bb = nb()
            def fn(e):
                for kc in range(KC):
                    inst = e.matmul(bk[:, 0:ncols], lhsT=lhs_fn(kc, blk), rhs=view[:, kc, 0:ncols],
                                    start=(kc == 0), stop=(kc == KC - 1))
                return inst
            S.op("tensor", fn, reads=[Bv] + list(rbufs), writes=[Bb])
            return bk, Bb

        def proj_ksplit(wap, KCT, kcp, col0, rhs_of_kc, rbufs, evac):
            nq = KCT // kcp
            bks = [nb() for _ in range(4)]
            for q in range(nq):
                view, Bv = load_panel(wap[q * kcp * 128:(q + 1) * kcp * 128, :], kcp, [(col0, col0 + 512)], rowq=q + 1)
                for n in range(4):
                    bk, Bb = bks[n]
                    def fn(e, q=q, n=n, bk=bk, view=view):
                        for kc in range(kcp):
                            inst = e.matmul(bk[:], lhsT=view[:, kc, n * 128:(n + 1) * 128], rhs=rhs_of_kc(q * kcp + kc),
                                            start=(q == 0 and kc == 0), stop=(q == nq - 1 and kc == kcp - 1))
                        return inst
                    S.op("tensor", fn, reads=[Bv] + list(rbufs), writes=[Bb])
                    flush_stats()
            for n in range(4):
                evac(n, bks[n][0], bks[n][1])

        hT_rhs = lambda kc: hT[:, kc, :]
        hT_lhs = lambda kc, blk: hT[:, kc, blk * 128:(blk + 1) * 128]

        xst = [PB[:, 21504 + i * 4096:21504 + (i + 1) * 4096].bitcast(F32) for i in range(2)]
        xpre = {}

        def x_prefetch(src, t):
            Bx = [Buf(f"xst{i}", gate=S.gate) for i in range(2)]
            for blk in range(2):
                r0 = t * T + blk * 128
                S.dma("sync", lambda e, blk=blk, r0=r0: e.dma_start(out=xst[blk], in_=src[r0:r0 + 128, :]), f"dx{blk}", writes=[Bx[blk]])
            xpre["next"] = Bx

        def load_x_tile(src, t):
            S.new_phase()
            pre = xpre.pop("next", None)
            Bxst = pre if pre is not None else [Buf(f"xst{i}", gate=S.gate) for i in range(2)]
            S.phase_bufs.extend(Bxst)
            for blk in range(4):
                r = blk % 2
                r0 = t * T + blk * 128
                if pre is None or blk >= 2:
                    S.dma("sync", lambda e, r=r, r0=r0: e.dma_start(out=xst[r], in_=src[r0:r0 + 128, :]), f"dx{r}", writes=[Bxst[r]])
                for g in range(4):
                    bk, Bb = nb()
                    def fn(e, r=r, g=g, bk=bk):
                        for j in range(4):
                            kc = g * 4 + j
                            inst = e.transpose(out=bk[:, j * 128:(j + 1) * 128], in_=xst[r][:, kc * 128:(kc + 1) * 128], identity=identf[:])
                        return inst
                    S.op("tensor", fn, reads=[Bxst[r], Bconst], writes=[Bb])
                    copy_op(ev_eng(), xT[:, g * 4:(g + 1) * 4, blk * 128:(blk + 1) * 128],
                            bk[:].rearrange("p (j c) -> p j c", j=4), [Bb], BxT[g * 4:(g + 1) * 4])

        def load_cs(which, t):
            S.dma("sync", [lambda e: e.dma_start(out=cst[:, 0, :], in_=cs_d[2 * which, :, t * T:(t + 1) * T]),
                           lambda e: e.dma_start(out=cst[:, 1, :], in_=cs_d[2 * which + 1, :, t * T:(t + 1) * T])],
                  "dcs", writes=[Bcst])

        stats = {"n": 0, "pend": []}
        sbank, Bsbank = banks[NBANK - 1], Bbank[NBANK - 1]

        def stats_chunk(kc):
            i = stats["n"]
            r = i % 4
            S.op("scalar", lambda e, kc=kc, r=r: e.activation(out=sq[:, r, :], in_=xT[:, kc, :], func=AF.Square),
                 reads=[BxT[kc]], writes=[Bsq[r]])
            def mm(i=i, r=r):
                S.op("tensor", lambda e: e.matmul(sbank[:], lhsT=ones[:], rhs=sq[:, r, :], start=(i == 0), stop=(i == 15)),
                     reads=[Bsq[r], Bconst], writes=([Bsbank] if i in (0, 15) else []))
            stats["pend"].append(mm)
            stats["n"] += 1

        def flush_stats():
            while stats["pend"]:
                stats["pend"].pop(0)()

        def norm(which, inplace=False):
            if stats["n"] == 0:
                for kc in range(16):
                    stats_chunk(kc)
                    flush_stats()
            flush_stats()
            assert stats["n"] == 16
            stats["n"] = 0
            S.op("scalar", lambda e: e.activation(out=rstd[:], in_=sbank[:], func=AF.Sqrt, bias=epsc[:, 0:1], scale=1.0),
                 reads=[Bsbank, Bconst], writes=[Brstd])
            S.op("vector", lambda e: e.reciprocal(out=rstd[:], in_=rstd[:]), reads=[Brstd], writes=[Brstd])
            for kc in range(16):
                if inplace:
                    S.op("vector", lambda e, kc=kc: e.scalar_tensor_tensor(out=xT[:, kc, :], in0=xT[:, kc, :], scalar=gcol(which, kc), in1=rstd[:], op0=ALU.mult, op1=ALU.mult),
                         reads=[Brstd, Bconst], writes=[BxT[kc]])
                else:
                    S.op("vector", lambda e, kc=kc: e.scalar_tensor_tensor(out=hT[:, kc, :], in0=xT[:, kc, :], scalar=gcol(which, kc), in1=rstd[:], op0=ALU.mult, op1=ALU.mult),
                         reads=[BxT[kc], Brstd, Bconst], writes=[BhT[kc]])

        def alloc_ret():
            R = {}
            R["m"] = PB[:, 0:16384].rearrange("p (c t) -> p c t", c=32)
            R["qT"] = PB[:, 16384:17408].rearrange("p (c t) -> p c t", c=2)
            R["kT"] = PB[:, 17408:18432].rearrange("p (c t) -> p c t", c=2)
            R["kz"] = PB[:, 18432:19456].rearrange("p (b d) -> p b d", b=4)
            R["v"] = PB[:, 19456:21504].rearrange("p (b e) -> p b e", b=4)
            R["sg"] = PB[:, 21504:23552].rearrange("p (c t) -> p c t", c=4)
            R["Sbf"] = [PB[:, 23552 + i * 1024:23552 + (i + 1) * 1024].rearrange("p (c e) -> p c e", c=2) for i in range(4)]
            pairs = [(kb, qb) for kb in range(4) for qb in range(kb, 4)]
            R["PT"] = {pr: PB[:, 27648 + i * 128:27648 + (i + 1) * 128] for i, pr in enumerate(pairs)}
            R["on"] = [PB[:, 28928 + i * 512:28928 + (i + 1) * 512] for i in range(4)]
            R["stage"] = [PF[:, i * 1024:(i + 1) * 1024].rearrange("p (c t) -> p c t", c=2) for i in range(2)]
            R["tmp"] = [PF[:, 2048 + i * 512:2048 + (i + 1) * 512] for i in range(4)]
            R["stats"] = [PF[:, 4096 + i * 16:4096 + (i + 1) * 16] for i in range(4)]
            B = {}
            B["m"] = [S.pbuf(f"m{i}") for i in range(32)]
            for k in ("qT", "kT", "kz", "v", "sg"):
                B[k] = S.pbuf(k)
            for k, n in (("on", 4), ("stage", 2), ("tmp", 4), ("stats", 4), ("Sbf", 4)):
                B[k] = [S.pbuf(f"{k}{i}") for i in range(n)]
            B["PT"] = {pr: S.pbuf(f"PT{pr}") for pr in pairs}
            return R, B

        def rotary(R, B, view, Bv, n0, dst, Bdst, sidx):
            stg, Bstg = R["stage"][sidx], B["stage"][sidx]
            for j in range(2):
                bk, Bb = group_fm(view, Bv, n0 + j, 16, hT_rhs, BhT)
                S.op("scalar", lambda e, j=j, bk=bk: e.activation(out=stg[:, j, :], in_=bk[:], func=AF.Copy), reads=[Bb], writes=[Bstg])
            tm, Bt = R["tmp"], B["tmp"]
            cos, sin = cst[:, 0, :], cst[:, 1, :]
            tt = lambda o, a, b_, op: (lambda e: e.tensor_tensor(out=o, in0=a, in1=b_, op=op))
            S.op("vector", tt(tm[0], stg[:, 0, :], cos, ALU.mult), reads=[Bstg, Bcst], writes=[Bt[0]])
            S.op("vector", tt(tm[1], stg[:, 1, :], sin, ALU.mult), reads=[Bstg, Bcst], writes=[Bt[1]])
            S.op("vector", tt(tm[2], stg[:, 1, :], cos, ALU.mult), reads=[Bstg, Bcst], writes=[Bt[2]])
            S.op("vector", tt(tm[3], stg[:, 0, :], sin, ALU.mult), reads=[Bstg, Bcst], writes=[Bt[3]])
            S.op("vector", tt(dst[:, 0, :], tm[0], tm[1], ALU.subtract), reads=[Bt[0], Bt[1]], writes=[Bdst])
            S.op("vector", tt(dst[:, 1, :], tm[2], tm[3], ALU.add), reads=[Bt[2], Bt[3]], writes=[Bdst])

        def k_tokmajor(R, B, h):
            for blk in range(4):
                bk, Bb = nb()
                bkb = bk[:].bitcast(BF16)
                def fn(e, blk=blk, bkb=bkb):
                    for dc in range(2):
                        inst = e.transpose(out=bkb[:, dc * 128:(dc + 1) * 128], in_=R["kT"][:, dc, blk * 128:(blk + 1) * 128], identity=identb[:])
                    return inst
                S.op("tensor", fn, reads=[B["kT"], Bconst], writes=[Bb])
                S.op("vector", lambda e, blk=blk, bkb=bkb: e.tensor_scalar(out=R["kz"][:, blk, :], in0=bkb[:, 0:256], scalar1=kzs4(h, blk), scalar2=None, op0=ALU.mult),
                     reads=[Bb, Bconst], writes=[B["kz"]])

        def v_proj(R, B, h, pre=None):
            view, Bv = pre if pre is not None else load_panel(w_in, 16, [(OVV + h * DV, OVV + (h + 1) * DV)])
            for blk in range(4):
                bk, Bb = group_tm(view, Bv, blk, 16, 512, hT_lhs, BhT)
                copy_op(ev_eng(), R["v"][:, blk, :], bk[:], [Bb], [B["v"]])

        def state_update_tile(R, B, h):
            for dc in range(2):
                bk, Bb = nb()
                def fn(e, dc=dc, bk=bk):
                    for blk in range(4):
                        inst = e.matmul(bk[:], lhsT=R["kz"][:, blk, dc * 128:(dc + 1) * 128], rhs=R["v"][:, blk, :], start=(blk == 0), stop=(blk == 3))
                    return inst
                S.op("tensor", fn, reads=[B["kz"], B["v"]], writes=[Bb])
                i = h * 2 + dc
                S.op("vector", lambda e, i=i, bk=bk: e.scalar_tensor_tensor(out=S32[:, i, :], in0=S32[:, i, :], scalar=float(gam128[h] ** 4), in1=bk[:], op0=ALU.mult, op1=ALU.add),
                     reads=[Bb], writes=[BS32[i]])

        pending = []

        def ret_head(R, B, h):
            view, Bv = load_panel(w_in, 16, [(OQ + h * DK, OQ + (h + 1) * DK)])
            i_kv = R["t"] * H + h
            S.dma("sync", lambda e: e.dma_start(out=PB[:, 17408:21504], in_=kvs_d[i_kv]), "dkv",
                  reads=[Bkvs[i_kv]], writes=[B["kT"], B["kz"], B["v"]])
            rotary(R, B, view, Bv, 0, R["qT"], B["qT"], 0)
            while pending:
                pending.pop(0)()
            view, Bv = load_panel(w_in, 16, [(OG + h * DV, OG + (h + 1) * DV)])
            for n in range(4):
                bk, Bb = group_fm(view, Bv, n, 16, hT_rhs, BhT)
                S.op("scalar", lambda e, n=n, bk=bk: e.activation(out=R["sg"][:, n, :], in_=bk[:], func=AF.Silu), reads=[Bb], writes=[B["sg"]])

            for blk in range(4):
                for dc in range(2):
                    S.op("scalar", lambda e, dc=dc, blk=blk: e.activation(out=R["Sbf"][blk][:, dc, :], in_=S32[:, h * 2 + dc, :], func=AF.Identity, scale=float(gam128[h] ** blk)),
                         reads=[BS32[h * 2 + dc]], writes=[B["Sbf"][blk]])
            for kb in range(4):
                nq = 4 - kb
                bks, Bbs = nb()
                def fsc(e, bks=bks, kb=kb, nq=nq):
                    for dc in range(2):
                        inst = e.matmul(bks[:, 0:nq * 128], lhsT=R["kT"][:, dc, kb * 128:(kb + 1) * 128], rhs=R["qT"][:, dc, kb * 128:512], start=(dc == 0), stop=(dc == 1))
                    return inst
                S.op("tensor", fsc, reads=[B["kT"], B["qT"]], writes=[Bbs])
                for qb in range(kb, 4):
                    d = qb - kb
                    pt, Bpt = R["PT"][(kb, qb)], B["PT"][(kb, qb)]
                    if d == 0:
                        S.op("vector", lambda e, bks=bks, pt=pt: e.tensor_tensor(out=pt, in0=bks[:, 0:128], in1=mask[:, h, :], op=ALU.mult),
                             reads=[Bbs, Bconst], writes=[Bpt])
                    else:
                        S.op("vector", lambda e, bks=bks, pt=pt, d=d: e.tensor_scalar(out=pt, in0=bks[:, d * 128:(d + 1) * 128], scalar1=offc(h, d), scalar2=None, op0=ALU.mult),
                             reads=[Bbs, Bconst], writes=[Bpt])
            state_update_tile(R, B, h)

            def fo_stage(blk):
                tsl = slice(blk * 128, (blk + 1) * 128)
                Sb, BSb = R["Sbf"][blk], B["Sbf"][blk]
                bko, Bbo = nb()
                def fo(e, bko=bko, tsl=tsl, blk=blk, Sb=Sb):
                    for kb in range(blk + 1):
                        e.matmul(bko[:], lhsT=R["PT"][(kb, blk)], rhs=R["v"][:, kb, :], start=(kb == 0), stop=False)
                    e.matmul(bko[:], lhsT=R["qT"][:, 0, tsl], rhs=Sb[:, 0, :], start=False, stop=False)
                    return e.matmul(bko[:], lhsT=R["qT"][:, 1, tsl], rhs=Sb[:, 1, :], start=False, stop=True)
                S.op("tensor", fo, reads=[B["PT"][(kb, blk)] for kb in range(blk + 1)] + [B["v"], B["qT"], BSb], writes=[Bbo])
                sts, Bst_ = R["stats"][blk], B["stats"][blk]
                S.op("vector", lambda e, bko=bko, sts=sts: e.bn_stats(out=sts[:, 0:6], in_=bko[:]), reads=[Bbo], writes=[Bst_])
                S.op("vector", lambda e, sts=sts: e.bn_aggr(out=sts[:, 8:10], in_=sts[:, 0:6]), reads=[Bst_], writes=[Bst_])
                S.op("scalar", lambda e, sts=sts: e.activation(out=sts[:, 10:11], in_=sts[:, 9:10], func=AF.Sqrt, bias=epsx(h), scale=1.0),
                     reads=[Bst_, Bconst], writes=[Bst_])
                S.op("vector", lambda e, sts=sts: e.reciprocal(out=sts[:, 10:11], in_=sts[:, 10:11]), reads=[Bst_], writes=[Bst_])
                S.op("vector", lambda e, bko=bko, sts=sts, blk=blk: e.tensor_scalar(out=R["on"][blk], in0=bko[:], scalar1=sts[:, 8:9], scalar2=sts[:, 10:11], op0=ALU.subtract, op1=ALU.mult),
                     reads=[Bbo, Bst_], writes=[B["on"][blk]])

                def tr_stage(blk=blk, tsl=tsl):
                    bkt, Bbt = nb()
                    bktb = bkt[:].bitcast(BF16)
                    def ftr(e, bktb=bktb):
                        for ec in range(4):
                            inst = e.transpose(out=bktb[:, ec * 128:(ec + 1) * 128], in_=R["on"][blk][:, ec * 128:(ec + 1) * 128], identity=identb[:])
                        return inst
                    S.op("tensor", ftr, reads=[B["on"][blk], Bconst], writes=[Bbt])
                    for ec in range(4):
                        j = h * 4 + ec
                        S.op("vector", lambda e, ec=ec, j=j, bktb=bktb: e.scalar_tensor_tensor(out=R["m"][:, j, tsl], in0=bktb[:, ec * 128:(ec + 1) * 128], scalar=gret(j), in1=R["sg"][:, ec, tsl], op0=ALU.mult, op1=ALU.mult),
                             reads=[Bbt, B["sg"], Bconst], writes=[B["m"][j]])
                pending.append(tr_stage)

            for blk in range(4):
                fo_stage(blk)

        Bkvs = [Buf(f"kvs{i}") for i in range(NT * H)]
        kvstore = []
        carry = {}

        def phaseA_tile(t):
            cur["ph"], cur["t"] = "A", t
            load_x_tile(x_d, t)
            load_cs(1, t)
            norm(0)
            S.new_phase()
            R, B = alloc_ret()
            R2 = dict(R)
            B2 = dict(B)
            R2["kT"] = PB[:, 0:1024].rearrange("p (c t) -> p c t", c=2)
            R2["kz"] = PB[:, 1024:2048].rearrange("p (b d) -> p b d", b=4)
            R2["v"] = PB[:, 2048:4096].rearrange("p (b e) -> p b e", b=4)
            for k in ("kT", "kz", "v"):
                B2[k] = S.pbuf(k + "_alt")
            regs = [PB[:, 17408:21504], PB[:, 0:4096]]
            for h in range(H):
                r = h % 2
                Rr, Br = (R, B) if r == 0 else (R2, B2)
                view, Bv = load_panel(w_in, 16, [(OK_ + h * DK, OK_ + (h + 1) * DK)])
                vpre = load_panel(w_in, 16, [(OVV + h * DV, OVV + (h + 1) * DV)])
                if t >= 1 and pre_list:
                    preconvert(*pre_list.pop(0))
                while kvstore:
                    kvstore.pop(0)()
                rotary(Rr, Br, view, Bv, 0, Rr["kT"], Br["kT"], 1)
                v_proj(Rr, Br, h, pre=vpre)
                k_tokmajor(Rr, Br, h)
                i = t * H + h
                def do_store(i=i, r=r, Br=Br):
                    S.dma("sync", lambda e: e.dma_start(out=kvs_d[i], in_=regs[r]), f"dks{r}",
                          reads=[Br["kT"], Br["kz"], Br["v"]], writes=[Bkvs[i]])
                kvstore.append(do_store)
                state_update_tile(Rr, Br, h)
                if t == NT - 1:
                    cst2 = PF[:, 4200:4232].rearrange("p (c t) -> p c t", c=16)
                    if h == 0:
                        carry["Bc2"] = S.pbuf("c2")
                    Bc2 = carry["Bc2"]
                    rhs2 = lambda kc: hT[:, kc, T - 2:T]
                    j = h % 4
                    if h < 4:
                        view, Bv = load_panel(w_in, 16, [(OC + j * 512, OC + (j + 1) * 512)])
                        for n in range(4):
                            bk, Bb = group_fm(view, Bv, n, 16, rhs2, BhT, N=2)
                            S.op("scalar", lambda e, c=j * 4 + n, bk=bk: e.activation(out=cst2[:, c, :], in_=bk[:, 0:2], func=AF.Copy), reads=[Bb], writes=[Bc2])
                    else:
                        view, Bv = load_panel(w_in, 16, [(OV + j * 512, OV + (j + 1) * 512)])
                        for n in range(4):
                            bk, Bb = group_fm(view, Bv, n, 16, rhs2, BhT, N=2)
                            S.op("vector", lambda e, c=j * 4 + n, bk=bk: e.tensor_tensor(out=ucarry[:, c, :], in0=bk[:, 0:2], in1=cst2[:, c, :], op=ALU.mult),
                                 reads=[Bb, Bc2], writes=[Bucar])
                if t == NT - 1 and h == 3:
                    exchange_send(0)
            while kvstore:
                kvstore.pop(0)()
            x_prefetch(x_d, t + 1 if t < NT - 1 else 0)

        xparts = []
        groups = [[0, 1], [2, 3], [4, 5], [6, 7]]

        def exchange_send(i):
            tin, tout = xch[i]
            Bxin, Bxout = Buf(f"xin{i}"), Buf(f"xout{i}")
            if i < 2:
                src = S32[:, i * 8:(i + 1) * 8, :]
                rb = BS32[i * 8:(i + 1) * 8]
                din_ap = tin.ap().rearrange("(p a) b -> p a b", a=8)
                dout_ap = tout.ap()[0:1024, :].rearrange("(p a) b -> p a b", a=8)
            else:
                src = ucarry[:].rearrange("p a b -> p (a b)")
                rb = [Bucar]
                din_ap = tin.ap()
                dout_ap = tout.ap()[0:128, :]
            S.dma("sync", lambda e: e.dma_start(out=din_ap, in_=src), f"dxi{i}", reads=rb, writes=[Bxin])
            S.cc("gpsimd", lambda e: e.collective_compute("AllGather", ALU.bypass, replica_groups=groups,
                                                          ins=[tin.ap().opt()], outs=[tout.ap().opt()]),
                 f"cc{i}", reads=[Bxin], writes=[Bxout])
            xparts.append((i, Bxout, src, rb, dout_ap))

        def exchange_recv(i):
            _, Bxout, src, rb, dout_ap = [p for p in xparts if p[0] == i][0]
            S.dma("sync", lambda e: e.dma_start(out=src, in_=dout_ap), f"dxo{i}", reads=[Bxout], writes=rb)
            flag = small[:, 160:161]
            if i < 2:
                for k in range(i * 8, (i + 1) * 8):
                    S.op("vector", lambda e, k=k: e.tensor_scalar(out=S32[:, k, :], in0=S32[:, k, :], scalar1=flag, scalar2=None, op0=ALU.mult),
                         reads=[Bconst], writes=[BS32[k]])
            else:
                ucf = ucarry[:].rearrange("p a b -> p (a b)")
                S.op("vector", lambda e: e.tensor_scalar(out=ucf, in0=ucf, scalar1=flag, scalar2=None, op0=ALU.mult),
                     reads=[Bconst], writes=[Bucar])

        out_toks = []

        def main_tile(t):
            cur["ph"], cur["t"] = "M", t
            load_x_tile(x_d, t)
            load_cs(1, t)
            norm(0)
            S.new_phase()
            R, B = alloc_ret()
            R["t"] = t
            for h in range(H):
                if t == 0 and h == 4:
                    exchange_recv(1)
                    exchange_recv(2)
                ret_head(R, B, h)
            while pending:
                pending.pop(0)()
            S.new_phase(keep=B["m"])
            mg = PB[:, 18432:26624].rearrange("p (c t) -> p c t", c=16)
            sgr = [PB[:, 16384:18432].rearrange("p (c t) -> p c t", c=4)]
            Bsgr = [S.pbuf("sgr0")]
            Bmg = [S.pbuf(f"mg{i}") for i in range(16)]
            m_rhs = lambda kc: R["m"][:, kc, :]
            for j in range(4):
                view, Bv = load_panel(w_in, 16, [(OGR + j * 512, OGR + (j + 1) * 512)])
                for n in range(4):
                    bk, Bb = group_fm(view, Bv, n, 16, hT_rhs, BhT)
                    S.op("scalar", lambda e, n=n, bk=bk: e.activation(out=sgr[0][:, n, :], in_=bk[:], func=AF.Sigmoid), reads=[Bb], writes=[Bsgr[0]])
                def ev_ro(n, bk, Bb, j=j):
                    c = j * 4 + n
                    S.op("vector", lambda e, c=c, n=n, bk=bk: e.tensor_tensor(out=mg[:, c, :], in0=bk[:], in1=sgr[0][:, n, :], op=ALU.mult),
                         reads=[Bb, Bsgr[0]], writes=[Bmg[c]])
                proj_ksplit(w_ro, 32, 16, j * 512, m_rhs, B["m"], ev_ro)
            S.new_phase(keep=Bmg)
            z = PB[:, 0:8192].rearrange("p (c t) -> p c t", c=16)
            Bz = [S.pbuf(f"z{i}") for i in range(16)]
            sgc = PB[:, 8192:10240].rearrange("p (c t) -> p c t", c=4)
            Bsgc = S.pbuf("sgc")
            Bst = PF[:, 0:2048].rearrange("p (c t) -> p c t", c=4)
            BBst = S.pbuf("Bst")
            uh = PF[:, 2048:2048 + 4 * 514].rearrange("p (c t) -> p c t", c=4)
            Buh = [S.pbuf(f"uh{i}") for i in range(4)]
            ytmp = PF[:, 4104:4616]
            Byt = S.pbuf("ytmp")
            for j in range(4):
                view, Bv = load_panel(w_in, 16, [(OB + j * 512, OB + (j + 1) * 512)])
                for n in range(4):
                    bk, Bb = group_fm(view, Bv, n, 16, hT_rhs, BhT)
                    S.op("scalar", lambda e, n=n, bk=bk: e.activation(out=Bst[:, n, :], in_=bk[:], func=AF.Copy), reads=[Bb], writes=[BBst])
                view, Bv = load_panel(w_in, 16, [(OC + j * 512, OC + (j + 1) * 512)])
                for n in range(4):
                    bk, Bb = group_fm(view, Bv, n, 16, hT_rhs, BhT)
                    S.op("scalar", lambda e, n=n, bk=bk: e.activation(out=uh[:, n, 2:514], in_=bk[:], func=AF.Copy), reads=[Bb], writes=[Buh[n]])
                view, Bv = load_panel(w_in, 16, [(OV + j * 512, OV + (j + 1) * 512)])
                for n in range(4):
                    c = j * 4 + n
                    bk, Bb = group_fm(view, Bv, n, 16, hT_rhs, BhT)
                    S.op("vector", lambda e, n=n, bk=bk: e.tensor_tensor(out=uh[:, n, 2:514], in0=bk[:], in1=uh[:, n, 2:514], op=ALU.mult),
                         reads=[Bb], writes=[Buh[n]])
                    S.op("scalar", lambda e, n=n, c=c: e.activation(out=uh[:, n, 0:2], in_=ucarry[:, c, :], func=AF.Copy), reads=[Bucar], writes=[Buh[n]])
                    S.op("scalar", lambda e, n=n, c=c: e.activation(out=ytmp, in_=uh[:, n, 0:512], func=AF.Identity, scale=cwc(0, c)),
                         reads=[Buh[n], Bconst], writes=[Byt])
                    S.op("scalar", lambda e, n=n, c=c: e.activation(out=ucarry[:, c, :], in_=uh[:, n, 512:514], func=AF.Copy), reads=[Buh[n]], writes=[Bucar])
                    S.op("vector", lambda e, n=n, c=c: e.scalar_tensor_tensor(out=ytmp, in0=uh[:, n, 1:513], scalar=cwc(1, c), in1=ytmp, op0=ALU.mult, op1=ALU.add),
                         reads=[Buh[n], Bconst], writes=[Byt])
                    S.op("vector", lambda e, n=n, c=c: e.scalar_tensor_tensor(out=ytmp, in0=uh[:, n, 2:514], scalar=cwc(2, c), in1=ytmp, op0=ALU.mult, op1=ALU.add),
                         reads=[Buh[n], Bconst], writes=[Byt])
                    S.op("vector", lambda e, n=n, c=c: e.tensor_tensor(out=z[:, c, :], in0=ytmp, in1=Bst[:, n, :], op=ALU.mult),
                         reads=[Byt, BBst], writes=[Bz[c]])
            z_rhs = lambda kc: z[:, kc, :]
            tmpf = ytmp
            Btmpf = Byt
            for j in range(4):
                view, Bv = load_panel(w_in, 16, [(OGC + j * 512, OGC + (j + 1) * 512)])
                for n in range(4):
                    bk, Bb = group_fm(view, Bv, n, 16, hT_rhs, BhT)
                    S.op("scalar", lambda e, n=n, bk=bk: e.activation(out=sgc[:, n, :], in_=bk[:], func=AF.Sigmoid), reads=[Bb], writes=[Bsgc])
                view, Bv = load_panel(w_co, 16, [(j * 512, (j + 1) * 512)])
                for n in range(4):
                    c = j * 4 + n
                    bk, Bb = group_fm(view, Bv, n, 16, z_rhs, Bz)
                    S.op("vector", lambda e, n=n, bk=bk: e.tensor_tensor(out=tmpf, in0=bk[:], in1=sgc[:, n, :], op=ALU.mult),
                         reads=[Bb, Bsgc], writes=[Btmpf])
                    S.op("vector", lambda e, c=c: e.tensor_tensor(out=mg[:, c, :], in0=tmpf, in1=mg[:, c, :], op=ALU.add),
                         reads=[Btmpf], writes=[Bmg[c]])
            mg_rhs = lambda kc: mg[:, kc, :]
            for j in range(4):
                view, Bv = load_panel(w_o, 16, [(j * 512, (j + 1) * 512)])
                for n in range(4):
                    c = j * 4 + n
                    bk, Bb = group_fm(view, Bv, n, 16, mg_rhs, Bmg)
                    S.op("vector", lambda e, c=c, bk=bk: e.tensor_tensor(out=xT[:, c, :], in0=bk[:], in1=xT[:, c, :], op=ALU.add),
                         reads=[Bb], writes=[BxT[c]])
                    stats_chunk(c)
            norm(1)
            S.new_phase()
            act = PB[:, 0:22528].rearrange("p (c t) -> p c t", c=44)
            Bact = [S.pbuf(f"act{i}") for i in range(44)]
            sa = PF[:, 0:2048].rearrange("p (c t) -> p c t", c=4)
            Bsa = S.pbuf("sa")
            for j in range(11):
                view, Bv = load_panel(w_fi, 16, [(j * 512, (j + 1) * 512)])
                for n in range(4):
                    bk, Bb = group_fm(view, Bv, n, 16, hT_rhs, BhT)
                    S.op("scalar", lambda e, n=n, bk=bk: e.activation(out=sa[:, n, :], in_=bk[:], func=AF.Silu), reads=[Bb], writes=[Bsa])
                view, Bv = load_panel(w_fi, 16, [(DFF + j * 512, DFF + (j + 1) * 512)])
                for n in range(4):
                    c = j * 4 + n
                    bk, Bb = group_fm(view, Bv, n, 16, hT_rhs, BhT)
                    S.op("vector", lambda e, n=n, c=c, bk=bk: e.tensor_tensor(out=act[:, c, :], in0=bk[:], in1=sa[:, n, :], op=ALU.mult),
                         reads=[Bb, Bsa], writes=[Bact[c]])
            act_rhs = lambda kc: act[:, kc, :]
            for j in range(4):
                def ev_fo(n, bk, Bb, j=j):
                    c = j * 4 + n
                    S.op("vector", lambda e, c=c, bk=bk: e.tensor_tensor(out=xT[:, c, :], in0=bk[:], in1=xT[:, c, :], op=ALU.add),
                         reads=[Bb], writes=[BxT[c]])
                    stats_chunk(c)
                proj_ksplit(w_fo, 44, 11, j * 512, act_rhs, Bact, ev_fo)
            norm(2)
            S.new_phase()
            pT = PB[:, 0:1024].rearrange("p (c t) -> p c t", c=2)
            BpT = S.pbuf("pT")
            sgp = PF[:, 0:2048].rearrange("p (c t) -> p c t", c=4)
            Bsgp = S.pbuf("sgp")
            tmp2 = PF[:, 2048:2560]
            Btmp2 = S.pbuf("tmp2")
            pst = PF[:, 2560:3584].rearrange("p (b f) -> p b f", b=4)
            Bpst = S.pbuf("pst")
            S.dma("sync", lambda e: e.dma_start(out=pst, in_=p_d[t * T:(t + 1) * T, :].rearrange("(b p) f -> p b f", p=128)), "dp", writes=[Bpst])
            for blk in range(4):
                bk, Bb = nb()
                def fpt(e, blk=blk, bk=bk):
                    for fc in range(2):
                        inst = e.transpose(out=bk[:, fc * 128:(fc + 1) * 128], in_=pst[:, blk, fc * 128:(fc + 1) * 128], identity=identf[:])
                    return inst
                S.op("tensor", fpt, reads=[Bpst, Bconst], writes=[Bb])
                copy_op(ev_eng(), pT[:, :, blk * 128:(blk + 1) * 128], bk[:, 0:256].rearrange("p (c t) -> p c t", c=2), [Bb], [BpT])
            vpp = PB[:, 1024:5120].rearrange("p (kc n) -> p kc n", kc=2)
            Bvpp = S.pbuf("wpp")
            S.dma("gpsimd", lambda e: e.dma_start(out=vpp, in_=w_pp.rearrange("(kc p) n -> p kc n", p=128)), "dwp", writes=[Bvpp])
            ppf = PB[:, 5120:21504].bitcast(F32).rearrange("p (c t) -> p c t", c=16)
            Bppf = [S.pbuf(f"ppf{i}") for i in range(16)]
            for c in range(16):
                bk, Bb = group_fm(vpp, Bvpp, c, 2, lambda kc: pT[:, kc, :], [BpT])
                copy_op(ev_eng(), ppf[:, c, :], bk[:], [Bb], [Bppf[c]])
            sgp1 = [PF[:, i * 512:(i + 1) * 512] for i in range(4)]
            Bsgp1 = [S.pbuf(f"sgp{i}") for i in range(4)]
            for j in range(4):
                view, Bv = load_panel(w_pg, 16, [(j * 512, (j + 1) * 512)])
                for n in range(4):
                    c = j * 4 + n
                    bk, Bb = group_fm(view, Bv, n, 16, hT_rhs, BhT)
                    S.op("scalar", lambda e, n=n, bk=bk: e.activation(out=sgp1[n], in_=bk[:], func=AF.Sigmoid), reads=[Bb], writes=[Bsgp1[n]])
                    S.op("vector", lambda e, n=n, c=c: e.tensor_tensor(out=sgp1[n], in0=sgp1[n], in1=ppf[:, c, :], op=ALU.mult),
                         reads=[Bppf[c]], writes=[Bsgp1[n]])
                    S.op("vector", lambda e, n=n, c=c: e.tensor_tensor(out=xT[:, c, :], in0=sgp1[n], in1=xT[:, c, :], op=ALU.add),
                         reads=[Bsgp1[n]], writes=[BxT[c]])
                    stats_chunk(c)
            if t < NT - 1:
                x_prefetch(x_d, t + 1)
            norm(3, inplace=True)
            S.new_phase()
            ost = [PF[:, i * 2048:(i + 1) * 2048] for i in range(2)]
            Bost = [S.pbuf(f"ost{i}") for i in range(2)]
            for blk in range(4):
                r = blk % 2
                for g in range(4):
                    bk, Bb = nb()
                    def fo2(e, g=g, blk=blk, bk=bk):
                        for j in range(4):
                            kc = g * 4 + j
                            inst = e.transpose(out=bk[:, j * 128:(j + 1) * 128], in_=xT[:, kc, blk * 128:(blk + 1) * 128], identity=identf[:])
                        return inst
                    S.op("tensor", fo2, reads=BxT[g * 4:(g + 1) * 4] + [Bconst], writes=[Bb])
                    copy_op(ev_eng(), ost[r][:, g * 512:(g + 1) * 512], bk[:], [Bb], [Bost[r]])
                r0 = t * T + blk * 128
                tok = S.dma("sync", lambda e, r=r, r0=r0: e.dma_start(out=out_d[r0:r0 + 128, :], in_=ost[r]), f"do{r}", reads=[Bost[r]])
                out_toks.append(tok)

        for t in range(NT):
            phaseA_tile(t)
        exchange_send(1)
        exchange_send(2)
        exchange_recv(0)
        for t in range(NT):
            main_tile(t)
        S.wait_all("sync", out_toks)
        build_nc.info = dict(npan=len(scr_ids), counts={k: len(v) for k, v in S.prog.items()})
        S.emit(block)
    return nc


_CACHE = {}


def _consts():
    hh = np.arange(H, dtype=np.float64)
    lg = np.log1p(-np.exp2(-5.0 - hh))
    idx = np.arange(128, dtype=np.float64)
    c = idx[None, :]
    m = idx[:, None]
    same = (c // 64) == (m // 64)
    earlier = (m // 64) < (c // 64)
    maskT = np.zeros((128, H, 128), np.float64)
    for h in range(H):
        e_same = np.abs(c - m) - c - 1.0
        e_ear = -m - 1.0 + 0.0 * c
        val = np.where(same, np.exp(lg[h] * e_same), np.where(earlier, np.exp(lg[h] * e_ear), 0.0))
        maskT[:, h, :] = val * (DK ** -0.5)
    kzs4 = np.zeros((128, H, 4), np.float64)
    for b in range(4):
        kzs4[:, :, b] = np.exp(lg[None, :] * (127.0 - idx[:, None] + 128.0 * (3 - b))) * (DK ** -0.5)
    offc = np.zeros((128, H, 3), np.float64)
    for d in range(1, 4):
        offc[:, :, d - 1] = np.exp(lg[None, :] * (128.0 * d - idx[:, None] - 1.0)) * (DK ** -0.5)
    xi = np.exp(lg[None, :] * (idx[:, None] + 1.0))
    epsx = EPS / (xi * xi)
    gam128 = np.exp(lg * 128.0)
    return maskT.astype(np.float32), kzs4.reshape(128, H * 4).astype(np.float32), offc.reshape(128, H * 3).astype(np.float32), epsx.astype(np.float32), gam128


def _cs_table(pos):
    half = 128
    inv = (10000.0 ** (-np.arange(half, dtype=np.float32) / half)).astype(np.float32)
    ang = pos.astype(np.float32)[None, :] * inv[:, None]
    return np.cos(ang).astype(np.float32), np.sin(ang).astype(np.float32)


def kernel(x, p, g_mix, w_in, conv_w, w_conv_out, g_ret, w_ret_out, w_o,
           g_ffn, w_ffn_in, w_ffn_out, g_ple, w_ple_gate, w_ple_proj, g_final):
    f = lambda a: np.ascontiguousarray(np.asarray(a, dtype=np.float32))
    x = f(x); p = f(p)
    maskT, kzs4, offc, epsx, gam128 = _consts()
    if "nc" not in _CACHE:
        _CACHE["nc"] = build_nc(gam128)
    nc = _CACHE["nc"]
    small = np.zeros((128, 256), np.float32)
    for i, g in enumerate((g_mix[0], g_ffn[0], g_ple[0], g_final)):
        small[:, i * 16:(i + 1) * 16] = f(g).reshape(16, 128).T
    small[:, 64:96] = f(g_ret[0]).reshape(32, 128).T
    cw = f(conv_w[0])
    for tap in range(3):
        small[:, 96 + tap * 16:96 + (tap + 1) * 16] = cw[tap].reshape(16, 128).T
    small[:, 161:193] = kzs4
    small[:, 193:217] = offc
    small[:, 152:160] = epsx
    ident = np.eye(128, dtype=np.float32)
    shared = {
        "w_in": f(w_in[0]), "w_conv_out": f(w_conv_out[0]), "w_ret_out": f(w_ret_out[0]), "w_o": f(w_o[0]),
        "w_ffn_in": f(w_ffn_in[0]), "w_ffn_out": f(w_ffn_out[0]), "w_ple_gate": f(w_ple_gate[0]),
        "w_ple_proj": f(w_ple_proj[0]), "maskT": np.ascontiguousarray(maskT.reshape(128, H * 128)),
        "small": small, "ident": ident,
    }
    cs_lo = _cs_table(np.arange(0, TOK))
    cs_hi = _cs_table(np.arange(TOK, 2 * TOK))
    in_maps = []
    for c in range(8):
        b, half = c // 2, c % 2
        m = dict(shared)
        m["x"] = np.ascontiguousarray(x[b, half * TOK:(half + 1) * TOK])
        m["p"] = np.ascontiguousarray(p[0, b, half * TOK:(half + 1) * TOK])
        cm = cs_hi if half == 1 else cs_lo
        m["cs"] = np.ascontiguousarray(np.stack([cm[0], cm[1], cm[0], cm[1]]))
        sm = small.copy()
        sm[:, 160] = float(half)
        m["small"] = sm
        in_maps.append(m)
    res = run_bass_kernel_spmd(nc, in_maps, core_ids=list(range(8)))
    out = np.empty((4, 2 * TOK, D), np.float32)
    for c in range(8):
        b, half = c // 2, c % 2
        out[b, half * TOK:(half + 1) * TOK] = res.results[c]["out"]
    return out
```

```python
import numpy as np
import ml_dtypes
from contextlib import ExitStack
import concourse.bass as bass
import concourse.mybir as mybir
from concourse.bass_utils import run_bass_kernel_spmd

F32 = mybir.dt.float32
BF16 = mybir.dt.bfloat16
AF = mybir.ActivationFunctionType
ALU = mybir.AluOpType

D = 2048
T = 512
NT = 4
TOK = 2048
H = 8
DK = 256
DV = 512
DFF = 5632
PLE = 256
NIN = 22528
EPS = 1e-6
OB, OC, OV, OQ, OK_, OVV, OG, OGC, OGR = 0, 2048, 4096, 6144, 8192, 10240, 14336, 18432, 20480
NSLOT = 2
NBANK = 8
NPAN = 112


class Buf:
    __slots__ = ("name", "w", "r", "const")

    def __init__(self, name, const=False, gate=()):
        self.name = name
        self.w = None
        self.r = list(gate)
        self.const = const


class Sched:
    ENGS = ("sync", "scalar", "vector", "gpsimd", "tensor")

    def __init__(self):
        self.prog = {e: [] for e in self.ENGS}
        self.sems = {}
        self.count = {}
        self.waited = {e: {} for e in self.ENGS}
        self.phase_bufs = []
        self.gate = []

    def add_sem(self, key, handle):
        self.sems[key] = handle
        self.count[key] = 0

    def _deps(self, eng, reads, writes, extra=()):
        deps = {}

        def add(tok):
            if tok is None:
                return
            k, v = tok
            if deps.get(k, 0) < v:
                deps[k] = v
        for b in reads:
            add(b.w)
        for b in writes:
            add(b.w)
            for t in b.r:
                add(t)
        for t in extra:
            add(t)
        waits = []
        wd = self.waited[eng]
        for k, v in deps.items():
            if wd.get(k, 0) < v:
                wd[k] = v
                waits.append((k, v))
        return waits

    def _mark(self, tok, reads, writes):
        for b in reads:
            if not b.const:
                b.r.append(tok)
        for b in writes:
            b.w = tok
            b.r = []

    def op(self, eng, fn, reads=(), writes=(), extra=()):
        waits = self._deps(eng, reads, writes, extra)
        self.count[eng] += 1
        tok = (eng, self.count[eng])
        self.prog[eng].append((waits, [fn], (eng, 1)))
        self._mark(tok, reads, writes)
        return tok

    def dma(self, eng, fns, semkey, reads=(), writes=(), extra=()):
        waits = self._deps(eng, reads, writes, extra)
        if not isinstance(fns, (list, tuple)):
            fns = [fns]
        self.count[semkey] += 16 * len(fns)
        tok = (semkey, self.count[semkey])
        self.prog[eng].append((waits, list(fns), (semkey, 16)))
        self._mark(tok, reads, writes)
        return tok

    def cc(self, eng, fn, semkey, reads=(), writes=()):
        waits = self._deps(eng, reads, writes)
        self.count[semkey] += 1
        tok = (semkey, self.count[semkey])
        self.prog[eng].append((waits, [fn], (semkey, 1)))
        self._mark(tok, reads, writes)
        return tok

    def wait_all(self, eng, toks):
        waits = self._deps(eng, (), (), toks)
        self.prog[eng].append((waits, [], None))

    def pbuf(self, name):
        b = Buf(name, gate=self.gate)
        self.phase_bufs.append(b)
        return b

    def new_phase(self, keep=()):
        mx = {}
        for b in self.phase_bufs:
            toks = list(b.r)
            if b.w is not None:
                toks.append(b.w)
            for k, v in toks:
                if mx.get(k, 0) < v:
                    mx[k] = v
        for k, v in self.gate:
            if mx.get(k, 0) < v:
                mx[k] = v
        self.gate = list(mx.items())
        self.phase_bufs = list(keep)

    def emit(self, block):
        S = self

        def mk(ename):
            def body(e):
                for waits, fns, inc in S.prog[ename]:
                    for k, v in waits:
                        e.wait_ge(S.sems[k], v)
                    for fn in fns:
                        inst = fn(e)
                        if inc is not None and (inc[1] == 16):
                            inst.then_inc(S.sems[inc[0]], 16)
                    if inc is not None and inc[1] == 1 and fns:
                        inst.then_inc(S.sems[inc[0]], 1)
            return body
        block.sync(mk("sync"))
        block.scalar(mk("scalar"))
        block.vector(mk("vector"))
        block.gpsimd(mk("gpsimd"))
        block.tensor(mk("tensor"))


def build_nc(gam128):
    nc = bass.Bass("TRN2", target_bir_lowering=False)

    def din(name, shape):
        return nc.dram_tensor(name, list(shape), F32, kind="ExternalInput").ap()
    x_d = din("x", [TOK, D])
    p_d = din("p", [TOK, PLE])
    w_in = din("w_in", [D, NIN])
    w_co = din("w_conv_out", [D, D])
    w_ro = din("w_ret_out", [H * DV, D])
    w_o = din("w_o", [D, D])
    w_fi = din("w_ffn_in", [D, 2 * DFF])
    w_fo = din("w_ffn_out", [DFF, D])
    w_pg = din("w_ple_gate", [D, D])
    w_pp = din("w_ple_proj", [PLE, D])
    cs_d = din("cs", [4, 128, TOK])
    mask_d = din("maskT", [128, H * 128])
    small_d = din("small", [128, 256])
    ident_d = din("ident", [128, 128])
    out_d = nc.dram_tensor("out", [TOK, D], F32, kind="ExternalOutput").ap()
    scr_d = nc.dram_tensor("scr", [NPAN, 128, 8192], BF16, kind="Internal").ap()
    kvs_d = nc.dram_tensor("kvs", [NT * H, 128, 4096], BF16, kind="Internal").ap()
    xch = [(nc.dram_tensor(f"xin{i}", [r, w], F32), nc.dram_tensor(f"xout{i}", [2 * r, w], F32)) for i, (r, w) in enumerate(((1024, 512), (1024, 512), (128, 32)))]

    with ExitStack() as es:
        def sb(name, shape, dt):
            return es.enter_context(nc.sbuf_tensor("sb_" + name, list(shape), dt))
        xT = sb("xT", [128, 16, T], F32)
        hT = sb("hT", [128, 16, T], BF16)
        wsl = [sb(f"wsl{i}", [128, 8192], BF16) for i in range(NSLOT)]
        S32 = sb("S32", [128, 16, 512], F32)
        cst = sb("cst", [128, 2, T], F32)
        mask = sb("mask", [128, H, 128], F32)
        small = sb("small", [128, 256], F32)
        identf = sb("identf", [128, 128], F32)
        identb = sb("identb", [128, 128], BF16)
        ones = sb("ones", [128, 128], BF16)
        sq = sb("sq", [128, 4, T], BF16)
        rstd = sb("rstd", [128, T], F32)
        ucarry = sb("ucarry", [128, 16, 2], F32)
        epsc = sb("epsc", [128, 2], F32)
        PB = sb("PB", [128, 30976], BF16)
        PF = sb("PF", [128, 4640], F32)
        banks = [es.enter_context(nc.psum_tensor(f"psbank{i}", [128, 512], F32)) for i in range(NBANK)]

        S = Sched()
        semnames = list(Sched.ENGS) + [f"dw{i}" for i in range(NSLOT)] + [f"dsw{i}" for i in range(NSLOT)] + ["dkv", "dks0", "dks1", "dxi0", "dxi1", "dxi2","dpc0", "dpc1", "dpc2", "dpc3", "dpc4", "dpc5", "dpc6", "dpc7", "dpc8", "dpc9", "dpc10", "dpc11", "dpc12", "dpc13", "dpc14", "dpc15", "dxo0", "dxo1", "dxo2", "cc0", "cc1", "cc2", "dx0", "dx1", "dc", "dcs", "dp", "do0", "do1", "dwp"]
        for k in semnames:
            S.add_sem(k, es.enter_context(nc.semaphore("sem_" + k)))
        block = es.enter_context(nc.Block())

        BxT = [Buf(f"xT{k}") for k in range(16)]
        BhT = [Buf(f"hT{k}") for k in range(16)]
        Bw = [Buf(f"w{i}") for i in range(NSLOT)]
        BS32 = [Buf(f"S32_{i}") for i in range(16)]
        Bcst = Buf("cst")
        Bconst = Buf("const", const=True)
        Bsq = [Buf(f"sq{i}") for i in range(4)]
        Brstd = Buf("rstd")
        Bucar = Buf("ucarry")
        Bbank = [Buf(f"bank{i}") for i in range(NBANK)]
        st = {"bank": 0, "slot": 0, "ev": 0}
        hctx = {"h": hT, "B": BhT, "cs": cst, "Bcs": Bcst}

        gcol = lambda which, kc: small[:, which * 16 + kc: which * 16 + kc + 1]
        gret = lambda j: small[:, 64 + j: 65 + j]
        cwc = lambda tap, kc: small[:, 96 + tap * 16 + kc: 97 + tap * 16 + kc]
        kzs4 = lambda h, b: small[:, 161 + h * 4 + b: 162 + h * 4 + b]
        offc = lambda h, d: small[:, 193 + h * 3 + d - 1: 194 + h * 3 + d - 1]
        epsx = lambda h: small[:, 152 + h: 153 + h]

        def nb():
            i = st["bank"]
            st["bank"] = (i + 1) % (NBANK - 1)
            return banks[i], Bbank[i]

        def ev_eng():
            st["ev"] ^= 1
            return "scalar" if st["ev"] else "vector"

        def copy_op(eng, out, in_, reads, writes):
            if eng == "scalar":
                return S.op("scalar", lambda e: e.activation(out=out, in_=in_, func=AF.Copy), reads=reads, writes=writes)
            return S.op("vector", lambda e: e.tensor_copy(out=out, in_=in_), reads=reads, writes=writes)

        S.dma("sync", [lambda e: e.dma_start(out=mask[:].rearrange("p h c -> p (h c)"), in_=mask_d),
                       lambda e: e.dma_start(out=small[:], in_=small_d),
                       lambda e: e.dma_start(out=identf[:], in_=ident_d)], "dc", writes=[Bconst])
        S.op("vector", lambda e: e.tensor_copy(out=identb[:], in_=identf[:]), reads=[Bconst], writes=[Bconst])
        S.op("vector", lambda e: e.memset(ones[:], 1.0 / D), writes=[Bconst])
        S.op("vector", lambda e: e.memset(epsc[:], EPS), writes=[Bconst])
        S.op("vector", lambda e: e.memset(S32[:].rearrange("p a b -> p (a b)"), 0.0), writes=BS32)
        S.op("vector", lambda e: e.memset(ucarry[:].rearrange("p a b -> p (a b)"), 0.0), writes=[Bucar])

        scr_ids = {}
        Bscr = {}
        scr_ok = {}
        wb_at = {}
        cur = {"ph": "A", "t": 0, "nmain": 0}

        def load_panel(wap, KC, colranges, rowq=0):
            s = st["slot"]
            st["slot"] = (s + 1) % NSLOT
            ncols = sum(c1 - c0 for c0, c1 in colranges)
            assert KC * ncols <= 8192
            flat = wsl[s][:, 0:KC * ncols]
            view = flat.rearrange("p (kc n) -> p kc n", kc=KC)
            key = (wap.name, rowq, KC, tuple(colranges))
            now = (cur["ph"], cur["t"])
            if key in scr_ids and scr_ok.get(key):
                pid = scr_ids[key]
                S.dma("sync", lambda e: e.dma_start(out=flat, in_=scr_d[pid, :, 0:KC * ncols]), f"dw{s}", reads=[Bscr[pid]], writes=[Bw[s]])
                return view, Bw[s]
            wv = wap.rearrange("(kc p) n -> p kc n", p=128)
            fns = []
            off = 0
            for c0, c1 in colranges:
                n = c1 - c0
                fns.append(lambda e, off=off, n=n, c0=c0, c1=c1: e.dma_start(out=view[:, :, off:off + n], in_=wv[:, :, c0:c1]))
                off += n
            S.dma("gpsimd", fns, f"dw{s}", writes=[Bw[s]])
            if key not in scr_ids:
                pid = len(scr_ids)
                assert pid < NPAN
                scr_ids[key] = pid
                Bscr[pid] = Buf(f"scr{pid}")
                tgt = now
                wb_at[key] = tgt
            if wb_at[key] == now:
                pid = scr_ids[key]
                S.dma("sync", lambda e: e.dma_start(out=scr_d[pid, :, 0:KC * ncols], in_=flat), f"dsw{s}", reads=[Bw[s]], writes=[Bscr[pid]])
                scr_ok[key] = True
            return view, Bw[s]

        npre = [0]

        def preconvert(wap, KC, colranges, rowq=0):
            key = (wap.name, rowq, KC, tuple(colranges))
            if key in scr_ids:
                return
            ncols = sum(c1 - c0 for c0, c1 in colranges)
            pid = len(scr_ids)
            assert pid < NPAN
            scr_ids[key] = pid
            Bscr[pid] = Buf(f"scr{pid}")
            wv = wap.rearrange("(kc p) n -> p kc n", p=128)
            dst = scr_d[pid, :, 0:KC * ncols].rearrange("p (kc n) -> p kc n", kc=KC)
            fns = []
            off = 0
            for c0, c1 in colranges:
                n = c1 - c0
                fns.append(lambda e, off=off, n=n, c0=c0, c1=c1: e.dma_start(out=dst[:, :, off:off + n], in_=wv[:, :, c0:c1]))
                off += n
            S.dma("gpsimd", fns, f"dpc{npre[0]}", writes=[Bscr[pid]])
            npre[0] += 1
            scr_ok[key] = True
            wb_at[key] = None

        pre_list = ([(w_in, 16, [(OC + j * 512, OC + (j + 1) * 512)]) for j in range(4)]
                    + [(w_in, 16, [(OV + j * 512, OV + (j + 1) * 512)]) for j in range(4)])

        def group_fm(view, Bv, n, KC, rhs_fn, rbufs, N=T):
            bk, Bb = nb()
            rhs_aps = [rhs_fn(kc) for kc in range(KC)]
            def fn(e):
                for kc in range(KC):
                    inst = e.matmul(bk[:, 0:N], lhsT=view[:, kc, n * 128:(n + 1) * 128], rhs=rhs_aps[kc],
                                    start=(kc == 0), stop=(kc == KC - 1))
                return inst
            S.op("tensor", fn, reads=[Bv] + list(rbufs), writes=[Bb])
            flush_stats()
            return bk, Bb

        def group_tm(view, Bv, blk, KC, ncols, lhs_fn, rbufs):
            bk, Bb = nb()
            lhs_aps = [lhs_fn(kc, blk) for kc in range(KC)]
            def fn(e):
                for kc in range(KC):
                    inst = e.matmul(bk[:, 0:ncols], lhsT=lhs_aps[kc], rhs=view[:, kc, 0:ncols],
                                    start=(kc == 0), stop=(kc == KC - 1))
                return inst
            S.op("tensor", fn, reads=[Bv] + list(rbufs), writes=[Bb])
            return bk, Bb

        def proj_ksplit(wap, KCT, kcp, col0, rhs_of_kc, rbufs, evac):
            nq = KCT // kcp
            bks = [nb() for _ in range(4)]
            for q in range(nq):
                view, Bv = load_panel(wap[q * kcp * 128:(q + 1) * kcp * 128, :], kcp, [(col0, col0 + 512)], rowq=q + 1)
                for n in range(4):
                    bk, Bb = bks[n]
                    def fn(e, q=q, n=n, bk=bk, view=view):
                        for kc in range(kcp):
                            inst = e.matmul(bk[:], lhsT=view[:, kc, n * 128:(n + 1) * 128], rhs=rhs_of_kc(q * kcp + kc),
                                            start=(q == 0 and kc == 0), stop=(q == nq - 1 and kc == kcp - 1))
                        return inst
                    S.op("tensor", fn, reads=[Bv] + list(rbufs), writes=[Bb])
                    flush_stats()
            for n in range(4):
                evac(n, bks[n][0], bks[n][1])

        hT_rhs = lambda kc: hctx["h"][:, kc, :]
        hT_lhs = lambda kc, blk: hctx["h"][:, kc, blk * 128:(blk + 1) * 128]

        xst = [PB[:, 21504 + i * 4096:21504 + (i + 1) * 4096].bitcast(F32) for i in range(2)]
        xpre = {}

        Bxst_p = [Buf("xst0"), Buf("xst1")]

        def x_prefetch(src, t):
            for b in Bxst_p:
                b.r.extend(S.gate)
            for blk in range(2):
                r0 = t * T + blk * 128
                S.dma("sync", lambda e, blk=blk, r0=r0: e.dma_start(out=xst[blk], in_=src[r0:r0 + 128, :]), f"dx{blk}", writes=[Bxst_p[blk]])
            xpre["next"] = True

        def load_x_tile(src, t):
            S.new_phase()
            pre = xpre.pop("next", None)
            Bxst = Bxst_p
            if pre is None:
                for b in Bxst_p:
                    b.r.extend(S.gate)
            S.phase_bufs.extend(Bxst)
            for blk in range(4):
                r = blk % 2
                r0 = t * T + blk * 128
                if pre is None or blk >= 2:
                    S.dma("sync", lambda e, r=r, r0=r0: e.dma_start(out=xst[r], in_=src[r0:r0 + 128, :]), f"dx{r}", writes=[Bxst[r]])
                for g in range(4):
                    bk, Bb = nb()
                    def fn(e, r=r, g=g, bk=bk):
                        for j in range(4):
                            kc = g * 4 + j
                            inst = e.transpose(out=bk[:, j * 128:(j + 1) * 128], in_=xst[r][:, kc * 128:(kc + 1) * 128], identity=identf[:])
                        return inst
                    S.op("tensor", fn, reads=[Bxst[r], Bconst], writes=[Bb])
                    copy_op(ev_eng(), xT[:, g * 4:(g + 1) * 4, blk * 128:(blk + 1) * 128],
                            bk[:].rearrange("p (j c) -> p j c", j=4), [Bb], BxT[g * 4:(g + 1) * 4])

        def load_cs(which, t):
            dst, Bd = hctx["cs"], hctx["Bcs"]
            S.dma("sync", [lambda e: e.dma_start(out=dst[:, 0, :], in_=cs_d[2 * which, :, t * T:(t + 1) * T]),
                           lambda e: e.dma_start(out=dst[:, 1, :], in_=cs_d[2 * which + 1, :, t * T:(t + 1) * T])],
                  "dcs", writes=[Bd])

        stats = {"n": 0, "pend": []}
        sbank, Bsbank = banks[NBANK - 1], Bbank[NBANK - 1]

        def stats_chunk(kc):
            i = stats["n"]
            r = i % 4
            S.op("scalar", lambda e, kc=kc, r=r: e.activation(out=sq[:, r, :], in_=xT[:, kc, :], func=AF.Square),
                 reads=[BxT[kc]], writes=[Bsq[r]])
            def mm(i=i, r=r):
                S.op("tensor", lambda e: e.matmul(sbank[:], lhsT=ones[:], rhs=sq[:, r, :], start=(i == 0), stop=(i == 15)),
                     reads=[Bsq[r], Bconst], writes=([Bsbank] if i in (0, 15) else []))
            stats["pend"].append(mm)
            stats["n"] += 1

        def flush_stats():
            while stats["pend"]:
                stats["pend"].pop(0)()

        def norm(which, inplace=False):
            if stats["n"] == 0:
                for kc in range(16):
                    stats_chunk(kc)
                    flush_stats()
            flush_stats()
            assert stats["n"] == 16
            stats["n"] = 0
            S.op("scalar", lambda e: e.activation(out=rstd[:], in_=sbank[:], func=AF.Sqrt, bias=epsc[:, 0:1], scale=1.0),
                 reads=[Bsbank, Bconst], writes=[Brstd])
            S.op("vector", lambda e: e.reciprocal(out=rstd[:], in_=rstd[:]), reads=[Brstd], writes=[Brstd])
            for kc in range(16):
                if inplace:
                    S.op("vector", lambda e, kc=kc: e.scalar_tensor_tensor(out=xT[:, kc, :], in0=xT[:, kc, :], scalar=gcol(which, kc), in1=rstd[:], op0=ALU.mult, op1=ALU.mult),
                         reads=[Brstd, Bconst], writes=[BxT[kc]])
                else:
                    hdst = hctx["h"]
                    S.op("vector", lambda e, kc=kc, hdst=hdst: e.scalar_tensor_tensor(out=hdst[:, kc, :], in0=xT[:, kc, :], scalar=gcol(which, kc), in1=rstd[:], op0=ALU.mult, op1=ALU.mult),
                         reads=[BxT[kc], Brstd, Bconst], writes=[hctx["B"][kc]])

        def alloc_ret():
            R = {}
            R["m"] = PB[:, 0:16384].rearrange("p (c t) -> p c t", c=32)
            R["qT"] = PB[:, 16384:17408].rearrange("p (c t) -> p c t", c=2)
            R["kT"] = PB[:, 17408:18432].rearrange("p (c t) -> p c t", c=2)
            R["kz"] = PB[:, 18432:19456].rearrange("p (b d) -> p b d", b=4)
            R["v"] = PB[:, 19456:21504].rearrange("p (b e) -> p b e", b=4)
            R["sg"] = PB[:, 21504:23552].rearrange("p (c t) -> p c t", c=4)
            R["Sbf"] = [PB[:, 23552 + i * 1024:23552 + (i + 1) * 1024].rearrange("p (c e) -> p c e", c=2) for i in range(4)]
            pairs = [(kb, qb) for kb in range(4) for qb in range(kb, 4)]
            R["PT"] = {pr: PB[:, 27648 + i * 128:27648 + (i + 1) * 128] for i, pr in enumerate(pairs)}
            R["on"] = [PB[:, 28928 + i * 512:28928 + (i + 1) * 512] for i in range(4)]
            R["stage"] = [PF[:, i * 1024:(i + 1) * 1024].rearrange("p (c t) -> p c t", c=2) for i in range(2)]
            R["tmp"] = [PF[:, 2048 + i * 512:2048 + (i + 1) * 512] for i in range(4)]
            R["stats"] = [PF[:, 4096 + i * 16:4096 + (i + 1) * 16] for i in range(4)]
            B = {}
            B["m"] = [S.pbuf(f"m{i}") for i in range(32)]
            for k in ("qT", "kT", "kz", "v", "sg"):
                B[k] = S.pbuf(k)
            for k, n in (("on", 4), ("stage", 2), ("tmp", 4), ("stats", 4), ("Sbf", 4)):
                B[k] = [S.pbuf(f"{k}{i}") for i in range(n)]
            B["PT"] = {pr: S.pbuf(f"PT{pr}") for pr in pairs}
            return R, B

        def rotary(R, B, view, Bv, n0, dst, Bdst, sidx):
            stg, Bstg = R["stage"][sidx], B["stage"][sidx]
            for j in range(2):
                bk, Bb = group_fm(view, Bv, n0 + j, 16, hT_rhs, hctx["B"])
                S.op("scalar", lambda e, j=j, bk=bk: e.activation(out=stg[:, j, :], in_=bk[:], func=AF.Copy), reads=[Bb], writes=[Bstg])
            tm, Bt = R["tmp"], B["tmp"]
            cos, sin = hctx["cs"][:, 0, :], hctx["cs"][:, 1, :]
            Bcst = hctx["Bcs"]
            tt = lambda o, a, b_, op: (lambda e: e.tensor_tensor(out=o, in0=a, in1=b_, op=op))
            S.op("vector", tt(tm[0], stg[:, 0, :], cos, ALU.mult), reads=[Bstg, Bcst], writes=[Bt[0]])
            S.op("vector", tt(tm[1], stg[:, 1, :], sin, ALU.mult), reads=[Bstg, Bcst], writes=[Bt[1]])
            S.op("vector", tt(tm[2], stg[:, 1, :], cos, ALU.mult), reads=[Bstg, Bcst], writes=[Bt[2]])
            S.op("vector", tt(tm[3], stg[:, 0, :], sin, ALU.mult), reads=[Bstg, Bcst], writes=[Bt[3]])
            S.op("vector", tt(dst[:, 0, :], tm[0], tm[1], ALU.subtract), reads=[Bt[0], Bt[1]], writes=[Bdst])
            S.op("vector", tt(dst[:, 1, :], tm[2], tm[3], ALU.add), reads=[Bt[2], Bt[3]], writes=[Bdst])

        def k_tokmajor(R, B, h):
            for blk in range(4):
                bk, Bb = nb()
                bkb = bk[:].bitcast(BF16)
                def fn(e, blk=blk, bkb=bkb):
                    for dc in range(2):
                        inst = e.transpose(out=bkb[:, dc * 128:(dc + 1) * 128], in_=R["kT"][:, dc, blk * 128:(blk + 1) * 128], identity=identb[:])
                    return inst
                S.op("tensor", fn, reads=[B["kT"], Bconst], writes=[Bb])
                S.op("vector", lambda e, blk=blk, bkb=bkb: e.tensor_scalar(out=R["kz"][:, blk, :], in0=bkb[:, 0:256], scalar1=kzs4(h, blk), scalar2=None, op0=ALU.mult),
                     reads=[Bb, Bconst], writes=[B["kz"]])

        def v_proj(R, B, h, pre=None):
            view, Bv = pre if pre is not None else load_panel(w_in, 16, [(OVV + h * DV, OVV + (h + 1) * DV)])
            for blk in range(4):
                bk, Bb = group_tm(view, Bv, blk, 16, 512, hT_lhs, hctx["B"])
                copy_op(ev_eng(), R["v"][:, blk, :], bk[:], [Bb], [B["v"]])

        def state_update_tile(R, B, h):
            for dc in range(2):
                bk, Bb = nb()
                def fn(e, dc=dc, bk=bk):
                    for blk in range(4):
                        inst = e.matmul(bk[:], lhsT=R["kz"][:, blk, dc * 128:(dc + 1) * 128], rhs=R["v"][:, blk, :], start=(blk == 0), stop=(blk == 3))
                    return inst
                S.op("tensor", fn, reads=[B["kz"], B["v"]], writes=[Bb])
                i = h * 2 + dc
                S.op("vector", lambda e, i=i, bk=bk: e.scalar_tensor_tensor(out=S32[:, i, :], in0=S32[:, i, :], scalar=float(gam128[h] ** 4), in1=bk[:], op0=ALU.mult, op1=ALU.add),
                     reads=[Bb], writes=[BS32[i]])

        pending = []

        def ret_head(R, B, h):
            view, Bv = load_panel(w_in, 16, [(OQ + h * DK, OQ + (h + 1) * DK)])
            i_kv = R["t"] * H + h
            S.dma("sync", lambda e: e.dma_start(out=PB[:, 17408:21504], in_=kvs_d[i_kv]), "dkv",
                  reads=[Bkvs[i_kv]], writes=[B["kT"], B["kz"], B["v"]])
            rotary(R, B, view, Bv, 0, R["qT"], B["qT"], 0)
            while pending:
                pending.pop(0)()
            view, Bv = load_panel(w_in, 16, [(OG + h * DV, OG + (h + 1) * DV)])
            for n in range(4):
                bk, Bb = group_fm(view, Bv, n, 16, hT_rhs, BhT)
                S.op("scalar", lambda e, n=n, bk=bk: e.activation(out=R["sg"][:, n, :], in_=bk[:], func=AF.Silu), reads=[Bb], writes=[B["sg"]])

            for blk in range(4):
                for dc in range(2):
                    S.op("scalar", lambda e, dc=dc, blk=blk: e.activation(out=R["Sbf"][blk][:, dc, :], in_=S32[:, h * 2 + dc, :], func=AF.Identity, scale=float(gam128[h] ** blk)),
                         reads=[BS32[h * 2 + dc]], writes=[B["Sbf"][blk]])
            for kb in range(4):
                nq = 4 - kb
                bks, Bbs = nb()
                def fsc(e, bks=bks, kb=kb, nq=nq):
                    for dc in range(2):
                        inst = e.matmul(bks[:, 0:nq * 128], lhsT=R["kT"][:, dc, kb * 128:(kb + 1) * 128], rhs=R["qT"][:, dc, kb * 128:512], start=(dc == 0), stop=(dc == 1))
                    return inst
                S.op("tensor", fsc, reads=[B["kT"], B["qT"]], writes=[Bbs])
                for qb in range(kb, 4):
                    d = qb - kb
                    pt, Bpt = R["PT"][(kb, qb)], B["PT"][(kb, qb)]
                    if d == 0:
                        S.op("vector", lambda e, bks=bks, pt=pt: e.tensor_tensor(out=pt, in0=bks[:, 0:128], in1=mask[:, h, :], op=ALU.mult),
                             reads=[Bbs, Bconst], writes=[Bpt])
                    else:
                        S.op("vector", lambda e, bks=bks, pt=pt, d=d: e.tensor_scalar(out=pt, in0=bks[:, d * 128:(d + 1) * 128], scalar1=offc(h, d), scalar2=None, op0=ALU.mult),
                             reads=[Bbs, Bconst], writes=[Bpt])
            state_update_tile(R, B, h)

            def fo_stage(blk):
                tsl = slice(blk * 128, (blk + 1) * 128)
                Sb, BSb = R["Sbf"][blk], B["Sbf"][blk]
                bko, Bbo = nb()
                def fo(e, bko=bko, tsl=tsl, blk=blk, Sb=Sb):
                    for kb in range(blk + 1):
                        e.matmul(bko[:], lhsT=R["PT"][(kb, blk)], rhs=R["v"][:, kb, :], start=(kb == 0), stop=False)
                    e.matmul(bko[:], lhsT=R["qT"][:, 0, tsl], rhs=Sb[:, 0, :], start=False, stop=False)
                    return e.matmul(bko[:], lhsT=R["qT"][:, 1, tsl], rhs=Sb[:, 1, :], start=False, stop=True)
                S.op("tensor", fo, reads=[B["PT"][(kb, blk)] for kb in range(blk + 1)] + [B["v"], B["qT"], BSb], writes=[Bbo])
                sts, Bst_ = R["stats"][blk], B["stats"][blk]
                S.op("vector", lambda e, bko=bko, sts=sts: e.bn_stats(out=sts[:, 0:6], in_=bko[:]), reads=[Bbo], writes=[Bst_])
                S.op("vector", lambda e, sts=sts: e.bn_aggr(out=sts[:, 8:10], in_=sts[:, 0:6]), reads=[Bst_], writes=[Bst_])
                S.op("scalar", lambda e, sts=sts: e.activation(out=sts[:, 10:11], in_=sts[:, 9:10], func=AF.Sqrt, bias=epsx(h), scale=1.0),
                     reads=[Bst_, Bconst], writes=[Bst_])
                S.op("vector", lambda e, sts=sts: e.reciprocal(out=sts[:, 10:11], in_=sts[:, 10:11]), reads=[Bst_], writes=[Bst_])
                S.op("vector", lambda e, bko=bko, sts=sts, blk=blk: e.tensor_scalar(out=R["on"][blk], in0=bko[:], scalar1=sts[:, 8:9], scalar2=sts[:, 10:11], op0=ALU.subtract, op1=ALU.mult),
                     reads=[Bbo, Bst_], writes=[B["on"][blk]])

                def tr_stage(blk=blk, tsl=tsl):
                    bkt, Bbt = nb()
                    bktb = bkt[:].bitcast(BF16)
                    def ftr(e, bktb=bktb):
                        for ec in range(4):
                            inst = e.transpose(out=bktb[:, ec * 128:(ec + 1) * 128], in_=R["on"][blk][:, ec * 128:(ec + 1) * 128], identity=identb[:])
                        return inst
                    S.op("tensor", ftr, reads=[B["on"][blk], Bconst], writes=[Bbt])
                    for ec in range(4):
                        j = h * 4 + ec
                        S.op("vector", lambda e, ec=ec, j=j, bktb=bktb: e.scalar_tensor_tensor(out=R["m"][:, j, tsl], in0=bktb[:, ec * 128:(ec + 1) * 128], scalar=gret(j), in1=R["sg"][:, ec, tsl], op0=ALU.mult, op1=ALU.mult),
                             reads=[Bbt, B["sg"], Bconst], writes=[B["m"][j]])
                pending.append(tr_stage)

            for blk in range(4):
                fo_stage(blk)

        Bkvs = [Buf(f"kvs{i}") for i in range(NT * H)]
        kvstore = []
        carry = {}

        hT2 = PB[:, 0:8192].rearrange("p (c t) -> p c t", c=16)
        BhT2 = [Buf(f"hT2_{k}") for k in range(16)]
        cs2 = PB[:, 12288:14336].bitcast(F32).rearrange("p (c t) -> p c t", c=2)
        Bcs2 = Buf("cs2")
        hsets = [dict(h=hT, B=BhT, cs=cst, Bcs=Bcst), dict(h=hT2, B=BhT2, cs=cs2, Bcs=Bcs2)]

        def phaseA_super(sp):
            cur["ph"], cur["t"] = "A", 2 * sp
            for u in range(2):
                hctx.update(hsets[u])
                load_x_tile(x_d, 2 * sp + u)
                load_cs(1, 2 * sp + u)
                norm(0)
                if u == 0:
                    x_prefetch(x_d, 2 * sp + 1)
            S.new_phase()
            S.phase_bufs.extend(BhT2 + [Bcs2])
            R, B = alloc_ret()
            R2 = dict(R)
            B2 = dict(B)
            R2["kT"] = PB[:, 8192:9216].rearrange("p (c t) -> p c t", c=2)
            R2["kz"] = PB[:, 9216:10240].rearrange("p (b d) -> p b d", b=4)
            R2["v"] = PB[:, 10240:12288].rearrange("p (b e) -> p b e", b=4)
            for k in ("kT", "kz", "v"):
                B2[k] = S.pbuf(k + "_alt")
            regs = [PB[:, 17408:21504], PB[:, 8192:12288]]
            for h in range(H):
                view, Bv = load_panel(w_in, 16, [(OK_ + h * DK, OK_ + (h + 1) * DK)])
                vpre = load_panel(w_in, 16, [(OVV + h * DV, OVV + (h + 1) * DV)])
                if pre_list:
                    preconvert(*pre_list.pop(0))
                for u in range(2):
                    t = 2 * sp + u
                    hctx.update(hsets[u])
                    r = u
                    Rr, Br = (R, B) if r == 0 else (R2, B2)
                    while len(kvstore) > 1:
                        kvstore.pop(0)()
                    rotary(Rr, Br, view, Bv, 0, Rr["kT"], Br["kT"], 1)
                    v_proj(Rr, Br, h, pre=vpre)
                    k_tokmajor(Rr, Br, h)
                    i = t * H + h
                    def do_store(i=i, r=r, Br=Br):
                        S.dma("sync", lambda e: e.dma_start(out=kvs_d[i], in_=regs[r]), f"dks{r}",
                              reads=[Br["kT"], Br["kz"], Br["v"]], writes=[Bkvs[i]])
                    kvstore.append(do_store)
                    state_update_tile(Rr, Br, h)
                if sp == 1:
                    hctx.update(hsets[1])
                    hlast = hctx["h"]
                    cst2 = PF[:, 4200:4232].rearrange("p (c t) -> p c t", c=16)
                    if h == 0:
                        carry["Bc2"] = S.pbuf("c2")
                    Bc2 = carry["Bc2"]
                    rhs2 = lambda kc: hlast[:, kc, T - 2:T]
                    j = h % 4
                    if h < 4:
                        view, Bv = load_panel(w_in, 16, [(OC + j * 512, OC + (j + 1) * 512)])
                        for n in range(4):
                            bk, Bb = group_fm(view, Bv, n, 16, rhs2, hctx["B"], N=2)
                            S.op("scalar", lambda e, c=j * 4 + n, bk=bk: e.activation(out=cst2[:, c, :], in_=bk[:, 0:2], func=AF.Copy), reads=[Bb], writes=[Bc2])
                    else:
                        view, Bv = load_panel(w_in, 16, [(OV + j * 512, OV + (j + 1) * 512)])
                        for n in range(4):
                            bk, Bb = group_fm(view, Bv, n, 16, rhs2, hctx["B"], N=2)
                            S.op("vector", lambda e, c=j * 4 + n, bk=bk: e.tensor_tensor(out=ucarry[:, c, :], in0=bk[:, 0:2], in1=cst2[:, c, :], op=ALU.mult),
                                 reads=[Bb, Bc2], writes=[Bucar])
                if sp == 1 and h == 3:
                    exchange_send(0)
            while kvstore:
                kvstore.pop(0)()
            hctx.update(hsets[0])
            x_prefetch(x_d, 2 if sp == 0 else 0)

        xparts = []
        groups = [[0, 1], [2, 3], [4, 5], [6, 7]]

        def exchange_send(i):
            tin, tout = xch[i]
            Bxin, Bxout = Buf(f"xin{i}"), Buf(f"xout{i}")
            if i < 2:
                src = S32[:, i * 8:(i + 1) * 8, :]
                rb = BS32[i * 8:(i + 1) * 8]
                din_ap = tin.ap().rearrange("(p a) b -> p a b", a=8)
                dout_ap = tout.ap()[0:1024, :].rearrange("(p a) b -> p a b", a=8)
            else:
                src = ucarry[:].rearrange("p a b -> p (a b)")
                rb = [Bucar]
                din_ap = tin.ap()
                dout_ap = tout.ap()[0:128, :]
            S.dma("sync", lambda e: e.dma_start(out=din_ap, in_=src), f"dxi{i}", reads=rb, writes=[Bxin])
            S.cc("gpsimd", lambda e: e.collective_compute("AllGather", ALU.bypass, replica_groups=groups,
                                                          ins=[tin.ap().opt()], outs=[tout.ap().opt()]),
                 f"cc{i}", reads=[Bxin], writes=[Bxout])
            xparts.append((i, Bxout, src, rb, dout_ap))

        def exchange_recv(i):
            _, Bxout, src, rb, dout_ap = [p for p in xparts if p[0] == i][0]
            S.dma("sync", lambda e: e.dma_start(out=src, in_=dout_ap), f"dxo{i}", reads=[Bxout], writes=rb)
            flag = small[:, 160:161]
            if i < 2:
                for k in range(i * 8, (i + 1) * 8):
                    S.op("vector", lambda e, k=k: e.tensor_scalar(out=S32[:, k, :], in0=S32[:, k, :], scalar1=flag, scalar2=None, op0=ALU.mult),
                         reads=[Bconst], writes=[BS32[k]])
            else:
                ucf = ucarry[:].rearrange("p a b -> p (a b)")
                S.op("vector", lambda e: e.tensor_scalar(out=ucf, in0=ucf, scalar1=flag, scalar2=None, op0=ALU.mult),
                     reads=[Bconst], writes=[Bucar])

        out_toks = []

        def main_tile(t):
            cur["ph"], cur["t"] = "M", t
            load_x_tile(x_d, t)
            load_cs(1, t)
            norm(0)
            S.new_phase()
            R, B = alloc_ret()
            R["t"] = t
            for h in range(H):
                if t == 0 and h == 4:
                    exchange_recv(1)
                    exchange_recv(2)
                ret_head(R, B, h)
            while pending:
                pending.pop(0)()
            S.new_phase(keep=B["m"])
            mg = PB[:, 18432:26624].rearrange("p (c t) -> p c t", c=16)
            sgr = [PB[:, 16384:18432].rearrange("p (c t) -> p c t", c=4)]
            Bsgr = [S.pbuf("sgr0")]
            Bmg = [S.pbuf(f"mg{i}") for i in range(16)]
            m_rhs = lambda kc: R["m"][:, kc, :]
            for j in range(4):
                view, Bv = load_panel(w_in, 16, [(OGR + j * 512, OGR + (j + 1) * 512)])
                for n in range(4):
                    bk, Bb = group_fm(view, Bv, n, 16, hT_rhs, BhT)
                    S.op("scalar", lambda e, n=n, bk=bk: e.activation(out=sgr[0][:, n, :], in_=bk[:], func=AF.Sigmoid), reads=[Bb], writes=[Bsgr[0]])
                def ev_ro(n, bk, Bb, j=j):
                    c = j * 4 + n
                    S.op("vector", lambda e, c=c, n=n, bk=bk: e.tensor_tensor(out=mg[:, c, :], in0=bk[:], in1=sgr[0][:, n, :], op=ALU.mult),
                         reads=[Bb, Bsgr[0]], writes=[Bmg[c]])
                proj_ksplit(w_ro, 32, 16, j * 512, m_rhs, B["m"], ev_ro)
            S.new_phase(keep=Bmg)
            z = PB[:, 0:8192].rearrange("p (c t) -> p c t", c=16)
            Bz = [S.pbuf(f"z{i}") for i in range(16)]
            sgc = PB[:, 8192:10240].rearrange("p (c t) -> p c t", c=4)
            Bsgc = S.pbuf("sgc")
            Bst = PF[:, 0:2048].rearrange("p (c t) -> p c t", c=4)
            BBst = S.pbuf("Bst")
            uh = PF[:, 2048:2048 + 4 * 514].rearrange("p (c t) -> p c t", c=4)
            Buh = [S.pbuf(f"uh{i}") for i in range(4)]
            ytmp = PF[:, 4104:4616]
            Byt = S.pbuf("ytmp")
            for j in range(4):
                view, Bv = load_panel(w_in, 16, [(OB + j * 512, OB + (j + 1) * 512)])
                for n in range(4):
                    bk, Bb = group_fm(view, Bv, n, 16, hT_rhs, BhT)
                    S.op("scalar", lambda e, n=n, bk=bk: e.activation(out=Bst[:, n, :], in_=bk[:], func=AF.Copy), reads=[Bb], writes=[BBst])
                view, Bv = load_panel(w_in, 16, [(OC + j * 512, OC + (j + 1) * 512)])
                for n in range(4):
                    bk, Bb = group_fm(view, Bv, n, 16, hT_rhs, BhT)
                    S.op("scalar", lambda e, n=n, bk=bk: e.activation(out=uh[:, n, 2:514], in_=bk[:], func=AF.Copy), reads=[Bb], writes=[Buh[n]])
                view, Bv = load_panel(w_in, 16, [(OV + j * 512, OV + (j + 1) * 512)])
                for n in range(4):
                    c = j * 4 + n
                    bk, Bb = group_fm(view, Bv, n, 16, hT_rhs, BhT)
                    S.op("vector", lambda e, n=n, bk=bk: e.tensor_tensor(out=uh[:, n, 2:514], in0=bk[:], in1=uh[:, n, 2:514], op=ALU.mult),
                         reads=[Bb], writes=[Buh[n]])
                    S.op("scalar", lambda e, n=n, c=c: e.activation(out=uh[:, n, 0:2], in_=ucarry[:, c, :], func=AF.Copy), reads=[Bucar], writes=[Buh[n]])
                    S.op("scalar", lambda e, n=n, c=c: e.activation(out=ytmp, in_=uh[:, n, 0:512], func=AF.Identity, scale=cwc(0, c)),
                         reads=[Buh[n], Bconst], writes=[Byt])
                    S.op("scalar", lambda e, n=n, c=c: e.activation(out=ucarry[:, c, :], in_=uh[:, n, 512:514], func=AF.Copy), reads=[Buh[n]], writes=[Bucar])
                    S.op("vector", lambda e, n=n, c=c: e.scalar_tensor_tensor(out=ytmp, in0=uh[:, n, 1:513], scalar=cwc(1, c), in1=ytmp, op0=ALU.mult, op1=ALU.add),
                         reads=[Buh[n], Bconst], writes=[Byt])
                    S.op("vector", lambda e, n=n, c=c: e.scalar_tensor_tensor(out=ytmp, in0=uh[:, n, 2:514], scalar=cwc(2, c), in1=ytmp, op0=ALU.mult, op1=ALU.add),
                         reads=[Buh[n], Bconst], writes=[Byt])
                    S.op("vector", lambda e, n=n, c=c: e.tensor_tensor(out=z[:, c, :], in0=ytmp, in1=Bst[:, n, :], op=ALU.mult),
                         reads=[Byt, BBst], writes=[Bz[c]])
            z_rhs = lambda kc: z[:, kc, :]
            tmpf = ytmp
            Btmpf = Byt
            for j in range(4):
                view, Bv = load_panel(w_in, 16, [(OGC + j * 512, OGC + (j + 1) * 512)])
                for n in range(4):
                    bk, Bb = group_fm(view, Bv, n, 16, hT_rhs, BhT)
                    S.op("scalar", lambda e, n=n, bk=bk: e.activation(out=sgc[:, n, :], in_=bk[:], func=AF.Sigmoid), reads=[Bb], writes=[Bsgc])
                view, Bv = load_panel(w_co, 16, [(j * 512, (j + 1) * 512)])
                for n in range(4):
                    c = j * 4 + n
                    bk, Bb = group_fm(view, Bv, n, 16, z_rhs, Bz)
                    S.op("vector", lambda e, n=n, bk=bk: e.tensor_tensor(out=tmpf, in0=bk[:], in1=sgc[:, n, :], op=ALU.mult),
                         reads=[Bb, Bsgc], writes=[Btmpf])
                    S.op("vector", lambda e, c=c: e.tensor_tensor(out=mg[:, c, :], in0=tmpf, in1=mg[:, c, :], op=ALU.add),
                         reads=[Btmpf], writes=[Bmg[c]])
            mg_rhs = lambda kc: mg[:, kc, :]
            for j in range(4):
                view, Bv = load_panel(w_o, 16, [(j * 512, (j + 1) * 512)])
                for n in range(4):
                    c = j * 4 + n
                    bk, Bb = group_fm(view, Bv, n, 16, mg_rhs, Bmg)
                    S.op("vector", lambda e, c=c, bk=bk: e.tensor_tensor(out=xT[:, c, :], in0=bk[:], in1=xT[:, c, :], op=ALU.add),
                         reads=[Bb], writes=[BxT[c]])
                    stats_chunk(c)
            norm(1)
            S.new_phase()
            act = PB[:, 0:22528].rearrange("p (c t) -> p c t", c=44)
            Bact = [S.pbuf(f"act{i}") for i in range(44)]
            sa = PF[:, 0:2048].rearrange("p (c t) -> p c t", c=4)
            Bsa = S.pbuf("sa")
            for j in range(11):
                view, Bv = load_panel(w_fi, 16, [(j * 512, (j + 1) * 512)])
                for n in range(4):
                    bk, Bb = group_fm(view, Bv, n, 16, hT_rhs, BhT)
                    S.op("scalar", lambda e, n=n, bk=bk: e.activation(out=sa[:, n, :], in_=bk[:], func=AF.Silu), reads=[Bb], writes=[Bsa])
                view, Bv = load_panel(w_fi, 16, [(DFF + j * 512, DFF + (j + 1) * 512)])
                for n in range(4):
                    c = j * 4 + n
                    bk, Bb = group_fm(view, Bv, n, 16, hT_rhs, BhT)
                    S.op("vector", lambda e, n=n, c=c, bk=bk: e.tensor_tensor(out=act[:, c, :], in0=bk[:], in1=sa[:, n, :], op=ALU.mult),
                         reads=[Bb, Bsa], writes=[Bact[c]])
            act_rhs = lambda kc: act[:, kc, :]
            for j in range(4):
                def ev_fo(n, bk, Bb, j=j):
                    c = j * 4 + n
                    S.op("vector", lambda e, c=c, bk=bk: e.tensor_tensor(out=xT[:, c, :], in0=bk[:], in1=xT[:, c, :], op=ALU.add),
                         reads=[Bb], writes=[BxT[c]])
                    stats_chunk(c)
                proj_ksplit(w_fo, 44, 11, j * 512, act_rhs, Bact, ev_fo)
            norm(2)
            S.new_phase()
            pT = PB[:, 0:1024].rearrange("p (c t) -> p c t", c=2)
            BpT = S.pbuf("pT")
            sgp = PF[:, 0:2048].rearrange("p (c t) -> p c t", c=4)
            Bsgp = S.pbuf("sgp")
            tmp2 = PF[:, 2048:2560]
            Btmp2 = S.pbuf("tmp2")
            pst = PF[:, 2560:3584].rearrange("p (b f) -> p b f", b=4)
            Bpst = S.pbuf("pst")
            S.dma("sync", lambda e: e.dma_start(out=pst, in_=p_d[t * T:(t + 1) * T, :].rearrange("(b p) f -> p b f", p=128)), "dp", writes=[Bpst])
            for blk in range(4):
                bk, Bb = nb()
                def fpt(e, blk=blk, bk=bk):
                    for fc in range(2):
                        inst = e.transpose(out=bk[:, fc * 128:(fc + 1) * 128], in_=pst[:, blk, fc * 128:(fc + 1) * 128], identity=identf[:])
                    return inst
                S.op("tensor", fpt, reads=[Bpst, Bconst], writes=[Bb])
                copy_op(ev_eng(), pT[:, :, blk * 128:(blk + 1) * 128], bk[:, 0:256].rearrange("p (c t) -> p c t", c=2), [Bb], [BpT])
            vpp = PB[:, 1024:5120].rearrange("p (kc n) -> p kc n", kc=2)
            Bvpp = S.pbuf("wpp")
            S.dma("gpsimd", lambda e: e.dma_start(out=vpp, in_=w_pp.rearrange("(kc p) n -> p kc n", p=128)), "dwp", writes=[Bvpp])
            ppf = PB[:, 5120:21504].bitcast(F32).rearrange("p (c t) -> p c t", c=16)
            Bppf = [S.pbuf(f"ppf{i}") for i in range(16)]
            for c in range(16):
                bk, Bb = group_fm(vpp, Bvpp, c, 2, lambda kc: pT[:, kc, :], [BpT])
                copy_op(ev_eng(), ppf[:, c, :], bk[:], [Bb], [Bppf[c]])
            sgp1 = [PF[:, i * 512:(i + 1) * 512] for i in range(4)]
            Bsgp1 = [S.pbuf(f"sgp{i}") for i in range(4)]
            for j in range(4):
                view, Bv = load_panel(w_pg, 16, [(j * 512, (j + 1) * 512)])
                for n in range(4):
                    c = j * 4 + n
                    bk, Bb = group_fm(view, Bv, n, 16, hT_rhs, BhT)
                    S.op("scalar", lambda e, n=n, bk=bk: e.activation(out=sgp1[n], in_=bk[:], func=AF.Sigmoid), reads=[Bb], writes=[Bsgp1[n]])
                    S.op("vector", lambda e, n=n, c=c: e.tensor_tensor(out=sgp1[n], in0=sgp1[n], in1=ppf[:, c, :], op=ALU.mult),
                         reads=[Bppf[c]], writes=[Bsgp1[n]])
                    S.op("vector", lambda e, n=n, c=c: e.tensor_tensor(out=xT[:, c, :], in0=sgp1[n], in1=xT[:, c, :], op=ALU.add),
                         reads=[Bsgp1[n]], writes=[BxT[c]])
                    stats_chunk(c)
            if t < NT - 1:
                x_prefetch(x_d, t + 1)
            norm(3, inplace=True)
            S.new_phase()
            ost = [PF[:, i * 2048:(i + 1) * 2048] for i in range(2)]
            Bost = [S.pbuf(f"ost{i}") for i in range(2)]
            for blk in range(4):
                r = blk % 2
                for g in range(4):
                    bk, Bb = nb()
                    def fo2(e, g=g, blk=blk, bk=bk):
                        for j in range(4):
                            kc = g * 4 + j
                            inst = e.transpose(out=bk[:, j * 128:(j + 1) * 128], in_=xT[:, kc, blk * 128:(blk + 1) * 128], identity=identf[:])
                        return inst
                    S.op("tensor", fo2, reads=BxT[g * 4:(g + 1) * 4] + [Bconst], writes=[Bb])
                    copy_op(ev_eng(), ost[r][:, g * 512:(g + 1) * 512], bk[:], [Bb], [Bost[r]])
                r0 = t * T + blk * 128
                tok = S.dma("sync", lambda e, r=r, r0=r0: e.dma_start(out=out_d[r0:r0 + 128, :], in_=ost[r]), f"do{r}", reads=[Bost[r]])
                out_toks.append(tok)

        for sp in range(2):
            phaseA_super(sp)
        exchange_send(1)
        exchange_send(2)
        exchange_recv(0)
        for t in range(NT):
            main_tile(t)
        S.wait_all("sync", out_toks)
        build_nc.info = dict(npan=len(scr_ids), counts={k: len(v) for k, v in S.prog.items()})
        S.emit(block)
    return nc


_CACHE = {}


def _consts():
    hh = np.arange(H, dtype=np.float64)
    lg = np.log1p(-np.exp2(-5.0 - hh))
    idx = np.arange(128, dtype=np.float64)
    c = idx[None, :]
    m = idx[:, None]
    same = (c // 64) == (m // 64)
    earlier = (m // 64) < (c // 64)
    maskT = np.zeros((128, H, 128), np.float64)
    for h in range(H):
        e_same = np.abs(c - m) - c - 1.0
        e_ear = -m - 1.0 + 0.0 * c
        val = np.where(same, np.exp(lg[h] * e_same), np.where(earlier, np.exp(lg[h] * e_ear), 0.0))
        maskT[:, h, :] = val * (DK ** -0.5)
    kzs4 = np.zeros((128, H, 4), np.float64)
    for b in range(4):
        kzs4[:, :, b] = np.exp(lg[None, :] * (127.0 - idx[:, None] + 128.0 * (3 - b))) * (DK ** -0.5)
    offc = np.zeros((128, H, 3), np.float64)
    for d in range(1, 4):
        offc[:, :, d - 1] = np.exp(lg[None, :] * (128.0 * d - idx[:, None] - 1.0)) * (DK ** -0.5)
    xi = np.exp(lg[None, :] * (idx[:, None] + 1.0))
    epsx = EPS / (xi * xi)
    gam128 = np.exp(lg * 128.0)
    return maskT.astype(np.float32), kzs4.reshape(128, H * 4).astype(np.float32), offc.reshape(128, H * 3).astype(np.float32), epsx.astype(np.float32), gam128


def _cs_table(pos):
    half = 128
    inv = (10000.0 ** (-np.arange(half, dtype=np.float32) / half)).astype(np.float32)
    ang = pos.astype(np.float32)[None, :] * inv[:, None]
    return np.cos(ang).astype(np.float32), np.sin(ang).astype(np.float32)


def kernel(x, p, g_mix, w_in, conv_w, w_conv_out, g_ret, w_ret_out, w_o,
           g_ffn, w_ffn_in, w_ffn_out, g_ple, w_ple_gate, w_ple_proj, g_final):
    f = lambda a: np.ascontiguousarray(np.asarray(a, dtype=np.float32))
    x = f(x); p = f(p)
    maskT, kzs4, offc, epsx, gam128 = _consts()
    if "nc" not in _CACHE:
        _CACHE["nc"] = build_nc(gam128)
    nc = _CACHE["nc"]
    small = np.zeros((128, 256), np.float32)
    for i, g in enumerate((g_mix[0], g_ffn[0], g_ple[0], g_final)):
        small[:, i * 16:(i + 1) * 16] = f(g).reshape(16, 128).T
    small[:, 64:96] = f(g_ret[0]).reshape(32, 128).T
    cw = f(conv_w[0])
    for tap in range(3):
        small[:, 96 + tap * 16:96 + (tap + 1) * 16] = cw[tap].reshape(16, 128).T
    small[:, 161:193] = kzs4
    small[:, 193:217] = offc
    small[:, 152:160] = epsx
    ident = np.eye(128, dtype=np.float32)
    shared = {
        "w_in": f(w_in[0]), "w_conv_out": f(w_conv_out[0]), "w_ret_out": f(w_ret_out[0]), "w_o": f(w_o[0]),
        "w_ffn_in": f(w_ffn_in[0]), "w_ffn_out": f(w_ffn_out[0]), "w_ple_gate": f(w_ple_gate[0]),
        "w_ple_proj": f(w_ple_proj[0]), "maskT": np.ascontiguousarray(maskT.reshape(128, H * 128)),
        "small": small, "ident": ident,
    }
    cs_lo = _cs_table(np.arange(0, TOK))
    cs_hi = _cs_table(np.arange(TOK, 2 * TOK))
    in_maps = []
    for c in range(8):
        b, half = c // 2, c % 2
        m = dict(shared)
        m["x"] = np.ascontiguousarray(x[b, half * TOK:(half + 1) * TOK])
        m["p"] = np.ascontiguousarray(p[0, b, half * TOK:(half + 1) * TOK])
        cm = cs_hi if half == 1 else cs_lo
        m["cs"] = np.ascontiguousarray(np.stack([cm[0], cm[1], cm[0], cm[1]]))
        sm = small.copy()
        sm[:, 160] = float(half)
        m["small"] = sm
        in_maps.append(m)
    res = run_bass_kernel_spmd(nc, in_maps, core_ids=list(range(8)))
    out = np.empty((4, 2 * TOK, D), np.float32)
    for c in range(8):
        b, half = c // 2, c % 2
        out[b, half * TOK:(half + 1) * TOK] = res.results[c]["out"]
    return out
```
